# Optimizing a Trainium2 kernel written in Bass

```python
import jax, jax.numpy as jnp
from jax import lax
import numpy as np


D_MODEL = 1024
BATCH = 32
SEQ = 2048
DEPTH = 1
DEC_BATCH = 8
DEC_SEQ = 32
PAST_LEN = 2048

CHUNK = 64
N_LEFT_CHUNKS = 8
LEFT_WINDOW = N_LEFT_CHUNKS * CHUNK
BAND = LEFT_WINDOW + CHUNK
HEAD_DIM = 64
D_MIX = D_MODEL
D_A = D_MIX // 2
D_B = D_MIX - D_A
H_A = D_A // HEAD_DIM
H_B = D_B // HEAD_DIM
MAX_REL = 128
N_REL = 2 * MAX_REL + 1
Q_BLOCK = 128
RMS_EPS = 1e-6
MASK_VALUE = -1e30
IN_SIZES = [D_A, D_A, D_A, D_A, D_B, D_B, D_B, D_B, H_B]
D_IN = sum(IN_SIZES)
IN_SPLITS = [int(s) for s in np.cumsum(IN_SIZES)[:-1]]

kernel_name = "hymba_chunk_band_fox_stream_step"


def rmsnorm(x, g):
    xf = x.astype(jnp.float32)
    y = xf * lax.rsqrt(jnp.mean(xf * xf, axis=-1, keepdims=True) + RMS_EPS)
    return (y * g.astype(jnp.float32)).astype(x.dtype)


def project(h, w_in, b_f):
    B, S, _ = h.shape
    z = jnp.einsum('bsd,de->bse', h, w_in)
    qa, ka, va, ga, qb, kb, vb, gb, zf = jnp.split(z, IN_SPLITS, axis=-1)
    heads_a = lambda t: t.reshape(B, S, H_A, HEAD_DIM)
    heads_b = lambda t: t.reshape(B, S, H_B, HEAD_DIM)
    logf = jax.nn.log_sigmoid(zf.astype(jnp.float32) + b_f.astype(jnp.float32))
    return heads_a(qa), heads_a(ka), heads_a(va), ga, heads_b(qb), heads_b(kb), heads_b(vb), gb, logf


def rel_bias_lookup(rel_bias, rel):
    idx = jnp.clip(rel, -MAX_REL, MAX_REL) + MAX_REL
    return rel_bias[:, idx].astype(jnp.float32)


def band_attend(q, k, v, bias, valid):
    s = jnp.einsum('bqhd,bkhd->bhqk', q, k).astype(jnp.float32) * (HEAD_DIM ** -0.5) + bias[None]
    s = jnp.where(valid[None, None], s, MASK_VALUE)
    p = jax.nn.softmax(s, axis=-1)
    return jnp.einsum('bhqk,bkhd->bqhd', p.astype(v.dtype), v)


def forget_attend(q, k, v, cq, ck, valid):
    s = jnp.einsum('bqhd,bkhd->bhqk', q, k).astype(jnp.float32) * (HEAD_DIM ** -0.5)
    s = s + jnp.transpose(cq, (0, 2, 1))[..., :, None] - jnp.transpose(ck, (0, 2, 1))[..., None, :]
    s = jnp.where(valid[None, None], s, MASK_VALUE)
    p = jax.nn.softmax(s, axis=-1)
    return jnp.einsum('bhqk,bkhd->bqhd', p.astype(v.dtype), v)


def chunk_attn_prompt(q, k, v, rel_bias):
    B, S, H, D = q.shape
    nc = S // CHUNK
    pad = ((0, 0), (LEFT_WINDOW, 0), (0, 0), (0, 0))
    kp = jnp.pad(k, pad)
    vp = jnp.pad(v, pad)
    qc = jnp.swapaxes(q.reshape(B, nc, CHUNK, H, D), 0, 1)
    i = jnp.arange(CHUNK)[:, None]
    j = jnp.arange(BAND)[None, :]
    bias = rel_bias_lookup(rel_bias, i + LEFT_WINDOW - j)

    def one_chunk(args):
        n, qn = args
        kn = lax.dynamic_slice_in_dim(kp, n * CHUNK, BAND, axis=1)
        vn = lax.dynamic_slice_in_dim(vp, n * CHUNK, BAND, axis=1)
        valid = (n * CHUNK - LEFT_WINDOW + j) >= 0
        return band_attend(qn, kn, vn, bias, valid)

    out = lax.map(one_chunk, (jnp.arange(nc), qc))
    return jnp.swapaxes(out, 0, 1).reshape(B, S, H, D)


def chunk_attn_sample(q, k, v, cache_k, cache_v, rel_bias):
    T = q.shape[1]
    L = cache_k.shape[1]
    kc = jnp.concatenate([cache_k.astype(k.dtype), k], axis=1)
    vc = jnp.concatenate([cache_v.astype(v.dtype), v], axis=1)
    rel = jnp.arange(T)[:, None] + L - jnp.arange(L + T)[None, :]
    bias = rel_bias_lookup(rel_bias, rel)
    valid = jnp.ones((T, L + T), dtype=bool)
    return band_attend(q, kc, vc, bias, valid)


def forget_attn_prompt(q, k, v, logf):
    B, S, H, D = q.shape
    nb = S // Q_BLOCK
    c = jnp.cumsum(logf, axis=1)
    qb = jnp.swapaxes(q.reshape(B, nb, Q_BLOCK, H, D), 0, 1)
    cb = jnp.swapaxes(c.reshape(B, nb, Q_BLOCK, H), 0, 1)
    kpos = jnp.arange(S)

    def one_block(args):
        n, qn, cn = args
        qpos = n * Q_BLOCK + jnp.arange(Q_BLOCK)
        valid = kpos[None, :] <= qpos[:, None]
        return forget_attend(qn, k, v, cn, c, valid)

    out = lax.map(one_block, (jnp.arange(nb), qb, cb))
    return jnp.swapaxes(out, 0, 1).reshape(B, S, H, D)


def forget_attn_sample(q, k, v, logf, cache_k, cache_v, cache_logf):
    T = q.shape[1]
    P = cache_k.shape[1]
    kc = jnp.concatenate([cache_k.astype(k.dtype), k], axis=1)
    vc = jnp.concatenate([cache_v.astype(v.dtype), v], axis=1)
    c = jnp.cumsum(jnp.concatenate([cache_logf.astype(jnp.float32), logf], axis=1), axis=1)
    cq = c[:, P:]
    valid = jnp.arange(P + T)[None, :] <= (P + jnp.arange(T))[:, None]
    return forget_attend(q, kc, vc, cq, c, valid)


def merge_out(oa, ga, ob, gb, w_out):
    B, S = oa.shape[:2]
    ya = oa.reshape(B, S, D_A) * jax.nn.silu(ga)
    yb = ob.reshape(B, S, D_B) * jax.nn.silu(gb)
    return jnp.einsum('bse,ed->bsd', jnp.concatenate([ya, yb], axis=-1), w_out)


def setup_inputs(seed: int = 0) -> dict:
    key = jax.random.key(seed)
    ks = jax.random.split(key, 16)
    la = min(LEFT_WINDOW, PAST_LEN)
    f32 = jnp.float32
    return {
        "x_prompt": jax.random.normal(ks[0], (BATCH, SEQ, D_MODEL), f32),
        "x_sample": jax.random.normal(ks[1], (DEC_BATCH, DEC_SEQ, D_MODEL), f32),
        "cache_a_k": jax.random.normal(ks[2], (DEPTH, DEC_BATCH, la, H_A, HEAD_DIM), f32),
        "cache_a_v": jax.random.normal(ks[3], (DEPTH, DEC_BATCH, la, H_A, HEAD_DIM), f32),
        "cache_b_k": jax.random.normal(ks[4], (DEPTH, DEC_BATCH, PAST_LEN, H_B, HEAD_DIM), f32),
        "cache_b_v": jax.random.normal(ks[5], (DEPTH, DEC_BATCH, PAST_LEN, H_B, HEAD_DIM), f32),
        "cache_b_logf": jax.nn.log_sigmoid(1.0 + 0.5 * jax.random.normal(ks[6], (DEPTH, DEC_BATCH, PAST_LEN, H_B), f32)),
        "norm_gain": 1.0 + 0.01 * jax.random.normal(ks[7], (DEPTH, D_MODEL), f32),
        "w_in": jax.random.normal(ks[8], (DEPTH, D_MODEL, D_IN), f32) * D_MODEL ** -0.5,
        "b_forget": 1.0 + 0.1 * jax.random.normal(ks[9], (DEPTH, H_B), f32),
        "rel_bias": 0.1 * jax.random.normal(ks[10], (DEPTH, H_A, N_REL), f32),
        "w_out": jax.random.normal(ks[11], (DEPTH, D_MIX, D_MODEL), f32) * D_MIX ** -0.5,
        "final_gain": 1.0 + 0.01 * jax.random.normal(ks[12], (D_MODEL,), f32),
    }


def reference(x_prompt, x_sample, cache_a_k, cache_a_v, cache_b_k, cache_b_v, cache_b_logf,
              norm_gain, w_in, b_forget, rel_bias, w_out, final_gain):
    xp = x_prompt
    xs = x_sample
    a_k_p, a_v_p, b_k_p, b_v_p, b_f_p = [], [], [], [], []
    a_k_s, a_v_s, b_k_s, b_v_s, b_f_s = [], [], [], [], []
    keep_a = min(LEFT_WINDOW, xp.shape[1])
    for l in range(DEPTH):
        hp = rmsnorm(xp, norm_gain[l])
        qa, ka, va, ga, qb, kb, vb, gb, logf = project(hp, w_in[l], b_forget[l])
        oa = chunk_attn_prompt(qa, ka, va, rel_bias[l])
        ob = forget_attn_prompt(qb, kb, vb, logf)
        xp = xp + merge_out(oa, ga, ob, gb, w_out[l])
        a_k_p.append(ka[:, -keep_a:])
        a_v_p.append(va[:, -keep_a:])
        b_k_p.append(kb)
        b_v_p.append(vb)
        b_f_p.append(logf)

        hs = rmsnorm(xs, norm_gain[l])
        qa, ka, va, ga, qb, kb, vb, gb, logf = project(hs, w_in[l], b_forget[l])
        oa = chunk_attn_sample(qa, ka, va, cache_a_k[l], cache_a_v[l], rel_bias[l])
        ob = forget_attn_sample(qb, kb, vb, logf, cache_b_k[l], cache_b_v[l], cache_b_logf[l])
        xs = xs + merge_out(oa, ga, ob, gb, w_out[l])
        a_k_s.append(ka)
        a_v_s.append(va)
        b_k_s.append(kb)
        b_v_s.append(vb)
        b_f_s.append(logf)

    y_prompt = rmsnorm(xp, final_gain)
    y_sample = rmsnorm(xs, final_gain)
    return (y_prompt, y_sample,
            jnp.stack(a_k_p), jnp.stack(a_v_p), jnp.stack(b_k_p), jnp.stack(b_v_p), jnp.stack(b_f_p),
            jnp.stack(a_k_s), jnp.stack(a_v_s), jnp.stack(b_k_s), jnp.stack(b_v_s), jnp.stack(b_f_s))
```

```python
import contextlib
import numpy as np
import concourse.bass as bass
import concourse.mybir as mybir
from concourse.bass_utils import run_bass_kernel_spmd
from concourse.alu_op_type import AluOpType as ALU

F32 = mybir.dt.float32
BF16 = mybir.dt.bfloat16
AF = mybir.ActivationFunctionType

D = 1024
S = 2048
DIN = 4104
NCORES = 8
NSEQ_CORE = 4
TS = 32
QA0, KA0, VA0, GA0, QB0, KB0, VB0, GB0, ZF0 = 0, 512, 1024, 1536, 2048, 2560, 3072, 3584, 4096
MASKV = -30000.0
EPS = 1e-6

ENGS = ("pe", "act", "dve", "pool", "sp")


class Op:
    __slots__ = ("eng", "fn", "deps", "signal", "sem", "val", "is_dma", "waits")

    def __init__(self, eng, fn):
        self.eng = eng
        self.fn = fn
        self.deps = []
        self.signal = False
        self.sem = None
        self.val = 0
        self.is_dma = False
        self.waits = []


def _is_psum_key(b):
    if isinstance(b, tuple):
        return b[0] in ("PJ", "ST", "OT")
    return b in ("TR", "MISC")


class Prog:
    def __init__(self, prefix):
        self.prefix = prefix
        self.q = {e: [] for e in ENGS}
        self.writers = {}
        self.readers = {}
        self.dma_cnt = {}

    def add(self, eng, fn, reads=(), writes=(), dma=None):
        op = Op(eng, fn)
        if dma is not None:
            op.is_dma = True
            op.sem = self.prefix + "d_" + dma
            self.dma_cnt[op.sem] = self.dma_cnt.get(op.sem, 0) + 16
            op.val = self.dma_cnt[op.sem]
            op.signal = True
        else:
            op.sem = self.prefix + eng
        excl = [b for b in reads if _is_psum_key(b)]
        if excl:
            reads = [b for b in reads if not _is_psum_key(b)]
            writes = list(writes) + [b for b in excl if b not in writes]
        deps = {}
        for b in reads:
            for d in self.writers.get(b, {}).values():
                deps[id(d)] = d
        for b in writes:
            for d in self.writers.get(b, {}).values():
                deps[id(d)] = d
            for d in self.readers.get(b, {}).values():
                deps[id(d)] = d
        for d in deps.values():
            if eng == "pe" and (not op.is_dma) and d.sem == op.sem:
                continue
            op.deps.append(d)
            d.signal = True
        for b in reads:
            self.readers.setdefault(b, {})[op.sem] = op
        for b in writes:
            self.writers.setdefault(b, {})[op.sem] = op
        self.q[eng].append(op)
        return op

    def add_dma_barrier(self, eng):
        op = Op(eng, lambda e: None)
        op.sem = self.prefix + eng
        last = {}
        for e_ in ENGS:
            for o in self.q[e_]:
                if o.is_dma:
                    last[o.sem] = o
        op.deps = list(last.values())
        self.q[eng].append(op)
        return op

    def finalize(self):
        for e in ENGS:
            c = 0
            for op in self.q[e]:
                if op.is_dma:
                    continue
                if op.signal:
                    c += 1
                    op.val = c
        for e in ENGS:
            waited = {}
            for op in self.q[e]:
                need = {}
                for d in op.deps:
                    if need.get(d.sem, 0) < d.val:
                        need[d.sem] = d.val
                op.waits = []
                for s, v in need.items():
                    if waited.get(s, 0) < v:
                        waited[s] = v
                        op.waits.append((s, v))

    def sem_names(self):
        names = set(self.prefix + e for e in ENGS)
        names.update(self.dma_cnt.keys())
        return sorted(names)

    def emit(self, block, sems):
        decos = {"pe": block.tensor, "act": block.scalar, "dve": block.vector,
                 "pool": block.gpsimd, "sp": block.sync}
        for e in ENGS:
            ops = self.q[e]
            if not ops:
                continue

            def body(eng, ops=ops):
                for op in ops:
                    for s, v in op.waits:
                        eng.wait_ge(sems[s], v)
                    inst = op.fn(eng)
                    if op.signal:
                        inst.then_inc(sems[op.sem], 16 if op.is_dma else 1)

            decos[e](body)


def _dr(t, row0, nrows, col0, ncols, rowlen):
    return bass.AP(t, row0 * rowlen + col0, [[rowlen, nrows], [1, ncols]])


def _bmid(ap, n):
    a = [list(x) for x in ap.ap]
    return bass.AP(ap.tensor, ap.offset, [a[0], [0, n]] + a[1:])


DBG = {'level': 99, 'macros': 99}


def build_program(nseq, with_sample=True):
    nc = bass.Bass("TRN2", target_bir_lowering=False, dynamic_dma_scratch_size=256)
    es = contextlib.ExitStack()

    def din(name, shape):
        return nc.dram_tensor(name, shape, F32, kind="ExternalInput")

    def dout(name, shape):
        return nc.dram_tensor(name, shape, F32, kind="ExternalOutput")

    xp = din("xp", [nseq * S, D])
    xs = din("xs", [TS, D])
    cak = din("cak", [512, 512]); cav = din("cav", [512, 512])
    cbk = din("cbk", [S, 512]); cbv = din("cbv", [S, 512]); cbf = din("cbf", [S, 8])
    gain = din("gain", [1, D]); w_in = din("w_in", [D, DIN]); bfg = din("bfg", [1, 8])
    rb = din("rb", [8, 257]); w_out = din("w_out", [D, D]); fgn = din("fgn", [1, D])
    yp = dout("yp", [nseq * S, D]); ys = dout("ys", [TS, D])
    akp = dout("akp", [nseq * 512, 512]); avp = dout("avp", [nseq * 512, 512])
    bkp = dout("bkp", [nseq * S, 512]); bvp = dout("bvp", [nseq * S, 512]); bfp = dout("bfp", [nseq * S, 8])
    aks = dout("aks", [TS, 512]); avs = dout("avs", [TS, 512])
    bks = dout("bks", [TS, 512]); bvs = dout("bvs", [TS, 512]); bfs = dout("bfs", [TS, 8])
    rbpad = nc.dram_tensor("rbpad", [8, 392], F32, kind="Internal")

    def sb(name, shape, dt):
        return es.enter_context(nc.sbuf_tensor(name, shape, dt))

    def ps(name, shape, dt):
        return es.enter_context(nc.psum_tensor(name, shape, dt))

    Wbf = sb("Wbf", [128, 8, DIN], BF16)
    Wob = sb("Wob", [128, 8, D], BF16)
    NXB = 4
    XB = [sb("XB%d" % i, [128, D], F32) for i in range(NXB)]
    HB = sb("HB", [128, D], BF16)
    JUNK = sb("JUNK", [128, D], BF16)
    hT = sb("hT", [128, 8, 512], BF16)
    KaT = sb("KaT", [128, 4, 8 * 128], BF16)
    KbT = sb("KbT", [128, 4, 17 * 128], BF16)
    VA = sb("VA", [128, 8, 4, 192], BF16)
    VB = sb("VB", [128, 17, 4, 192], BF16)
    QE = [sb("QE%d" % i, [128, 512], BF16) for i in range(2)]
    QO = [sb("QO%d" % i, [128, 512], BF16) for i in range(2)]
    SG = sb("SG", [128, 8, 512], BF16)
    NSTG = 2
    STG = [sb("STG%d" % i, [128, 512], F32) for i in range(NSTG)]
    KTM = [sb("KTM%d" % i, [128, 512], BF16) for i in range(2)]
    NPT = 4
    PT = [sb("PT%d" % i, [128, 512], BF16) for i in range(NPT)]
    Y1 = [sb("Y1%d" % i, [128, 512], BF16) for i in range(1)]
    RR = [sb("RR%d" % i, [128, 512], F32) for i in range(1)]
    yT = sb("yT", [128, 8, 512], BF16)
    FG = sb("FG", [128, D], F32)
    TB01 = sb("TB01", [128, 8, 256], BF16)
    T4 = sb("T4", [128, 128], BF16)
    TDG = sb("TDG", [128, 128], BF16)
    Jb = sb("Jb", [128, 128], BF16)
    IDb = sb("IDb", [128, 128], BF16)
    ONES = sb("ONES", [128, 128], F32)
    UTRI = sb("UTRI", [128, 128], F32)
    SEL64 = sb("SEL64", [128, 128], F32)
    SELS = sb("SELS", [128, 128], F32)
    SEL2 = sb("SEL2", [128, 128], F32)
    M30 = sb("M30", [128, 128], F32)
    GT = sb("GT", [128, 8], F32)
    BFB = sb("BFB", [128, 8], F32)
    B256 = sb("B256", [128, 8], F32)
    EPSC = sb("EPSC", [128, 1], F32)
    RBP = sb("RBP", [8, 392], F32)
    SS = sb("SS", [128, 8], F32)
    LNT = sb("LNT", [128, 8], F32)
    RSTD = sb("RSTD", [128, 8], F32)
    ZT = sb("ZT", [128, 32], F32)
    ET = sb("ET", [128, 32], F32)
    LB = sb("LB", [128, 16, 8], F32)
    LSH = sb("LSH", [128, 17, 8], F32)
    LACC = sb("LACC", [128, 8], F32)
    CK = sb("CK", [128, 17, 8], F32)
    BIAS = sb("BIAS", [128, 17, 8], F32)
    ONESEL = sb("ONESEL", [128, 8, 128], BF16)
    DT = sb("DT", [128, 512], BF16)
    TMP8 = sb("TMP8", [128, 8], F32)
    E2 = sb("E2", [128, 2], F32)

    PJ = [ps("PJ%d" % i, [128, 512], F32) for i in range(2)]
    ST = [ps("ST%d" % i, [128, 512], F32) for i in range(2)]
    OT = [ps("OT%d" % i, [128, 512], F32) for i in range(2)]
    TR = ps("TR", [128, 1024], BF16)
    MISC = ps("MISC", [128, 512], F32)

    P0 = Prog("a")
    A0 = P0.add

    A0("pool", lambda e: e.memset(ONES[:], 1.0), writes=["ONES"])
    A0("pool", lambda e: e.memset(M30[:], MASKV), writes=["M30"])
    A0("pool", lambda e: e.memset(EPSC[:], EPS), writes=["EPSC"])
    A0("pool", lambda e: e.affine_select(out=IDb[:], in_=ONES[:], pattern=[[-1, 128]], compare_op=ALU.is_equal,
                                         fill=0.0, base=0, channel_multiplier=1), reads=["ONES"], writes=["IDb"])
    A0("pool", lambda e: e.affine_select(out=Jb[:], in_=ONES[:], pattern=[[1, 128]], compare_op=ALU.is_equal,
                                         fill=0.0, base=-127, channel_multiplier=1), reads=["ONES"], writes=["Jb"])
    A0("pool", lambda e: e.affine_select(out=UTRI[:], in_=ONES[:], pattern=[[1, 128]], compare_op=ALU.is_ge,
                                         fill=0.0, base=0, channel_multiplier=-1), reads=["ONES"], writes=["UTRI"])
    A0("pool", lambda e: e.affine_select(out=TDG[:], in_=M30[:], pattern=[[-1, 128]], compare_op=ALU.is_gt,
                                         fill=0.0, base=127, channel_multiplier=-1), reads=["M30"], writes=["TDG"])
    A0("pool", lambda e: e.memset(T4[:], 0.0), writes=["T4"])
    A0("pool", lambda e: e.memset(T4[64:128, 64:128], MASKV), writes=["T4"])
    A0("pool", lambda e: e.memset(SEL64[:], 0.0), writes=["SEL64"])
    A0("pool", lambda e: e.memset(SEL64[32:33, :], 1.0), writes=["SEL64"])
    A0("pool", lambda e: e.memset(E2[:], 0.0), writes=["E2"])
    A0("pool", lambda e: e.memset(E2[32:33, 0:1], 1.0), writes=["E2"])
    A0("pool", lambda e: e.memset(E2[96:97, 1:2], 1.0), writes=["E2"])
    A0("pool", lambda e: e.memset(SELS[:], 0.0), writes=["SELS"])
    A0("pool", lambda e: e.memset(SELS[0:1, :], 1.0), writes=["SELS"])
    A0("pool", lambda e: e.memset(SEL2[:], 0.0), writes=["SEL2"])
    A0("pool", lambda e: e.memset(SEL2[64:65, 0:64], 1.0), writes=["SEL2"])
    A0("pool", lambda e: e.memset(SEL2[0:1, 64:128], 1.0), writes=["SEL2"])
    A0("pool", lambda e: e.memset(KaT[:], 0.0), writes=["KaT"])
    A0("pool", lambda e: e.memset(KbT[:], 0.0), writes=["KbT"])
    A0("pool", lambda e: e.memset(VA[:], 0.0), writes=["VA"])
    A0("pool", lambda e: e.memset(VB[:], 0.0), writes=["VB"])
    A0("pool", lambda e: e.memset(VA[:, :, :, 64:65], 1.0), writes=["VA"])
    A0("pool", lambda e: e.memset(VB[:, :, :, 64:65], 1.0), writes=["VB"])
    for i in range(2):
        A0("pool", lambda e, i=i: e.memset(QE[i][:], 0.0), writes=["QE"])
        A0("pool", lambda e, i=i: e.memset(QO[i][:], 0.0), writes=["QO"])
        if i < len(RR):
            A0("pool", lambda e, i=i: e.memset(RR[i][:], 0.0), writes=["RR"])
    A0("pool", lambda e: e.memset(LB[:], 0.0), writes=["LB"])
    A0("pool", lambda e: e.memset(LSH[:], 0.0), writes=["LSH"])
    A0("pool", lambda e: e.memset(CK[:], 0.0), writes=["CK"])
    A0("pool", lambda e: e.memset(BIAS[:], 0.0), writes=["BIAS"])
    A0("pool", lambda e: e.memset(DT[:], 0.0), writes=["DT"])
    A0("pool", lambda e: e.memset(TMP8[:], 0.0), writes=["TMP8"])
    A0("pool", lambda e: e.affine_select(out=ONESEL[:], in_=bass.AP(ONES, 0, [[128, 128], [0, 8], [1, 128]]),
                                         pattern=[[-1, 8], [0, 128]], compare_op=ALU.is_equal, fill=0.0, base=0,
                                         channel_multiplier=1), reads=["ONES"], writes=["ONESEL"])

    for kc in range(8):
        A0("sp", lambda e, kc=kc: e.dma_start(out=GT[:, kc:kc + 1], in_=bass.AP(gain, kc * 128, [[1, 128], [1, 1]])),
           writes=["GT"], dma="gt")
    for h in range(8):
        A0("act", lambda e, h=h: e.dma_start(out=B256[:, h:h + 1], in_=bass.AP(rb, h * 257 + 256, [[0, 128], [1, 1]])),
           writes=["B256"], dma="b256")
    A0("act", lambda e: e.dma_start(out=BFB[:], in_=bass.AP(bfg, 0, [[0, 128], [1, 8]])), writes=["BFB"], dma="bfb")
    A0("act", lambda e: e.dma_start(out=FG[:], in_=bass.AP(fgn, 0, [[0, 128], [1, D]])), writes=["FG"], dma="fg")
    A0("act", lambda e: e.dma_start(out=RBP[:, 0:257], in_=rb.ap()), writes=["RBP"], dma="rbp0")
    A0("act", lambda e: e.activation(out=RBP[:, 257:392],
                                     in_=bass.AP(RBP[:, 256:257].tensor, RBP[:, 256:257].offset,
                                                 [list(RBP[:, 256:257].ap[0]), [0, 135]]), func=AF.Copy),
       reads=["RBP"], writes=["RBP2"])
    A0("act", lambda e: e.dma_start(out=rbpad.ap(), in_=RBP[:]), reads=["RBP", "RBP2"], writes=["rbpad"], dma="rbp")
    for h in range(8):
        A0("act", lambda e, h=h: e.dma_start(out=XB[0][:, h * 128:(h + 1) * 128],
                                            in_=bass.AP(rbpad, h * 392 + 1, [[1, 128], [1, 128]])),
           reads=["rbpad"], writes=["XB0"], dma="tb0")
        A0("act", lambda e, h=h: e.dma_start(out=XB[1][:, h * 128:(h + 1) * 128],
                                            in_=bass.AP(rbpad, h * 392 + 129, [[1, 128], [1, 128]])),
           reads=["rbpad"], writes=["XB1"], dma="tb1")
    pieces = [(0, 1024), (1024, 1024), (2048, 1024), (3072, 1024), (4096, 8)]
    cnt = 0
    rot = ["dve", "act", "dve", "act", "pool", "dve", "act"]

    def cast_op(eng, out_ap, in_ap, scale_ap, rk, wk):
        if eng == "act":
            if scale_ap is None:
                A0("act", lambda e: e.activation(out=out_ap, in_=in_ap, func=AF.Copy), reads=rk, writes=wk)
            else:
                A0("act", lambda e: e.activation(out=out_ap, in_=in_ap, func=AF.Copy, scale=scale_ap), reads=rk, writes=wk)
        elif scale_ap is None:
            A0(eng, lambda e: e.tensor_copy(out=out_ap, in_=in_ap), reads=rk, writes=wk)
        else:
            A0(eng, lambda e: e.tensor_scalar(out=out_ap, in0=in_ap, scalar1=scale_ap, scalar2=1.0, op0=ALU.mult,
                                              op1=ALU.mult), reads=rk, writes=wk)

    stage = [(XB[i][:, :], "XB%d" % i) for i in (2, 3)]
    for nm, t in (("hT", hT), ("yT", yT), ("SG", SG)):
        for hh in range(2):
            stage.append((t[:, 4 * hh:4 * hh + 4, :].rearrange("p a b -> p (a b)").bitcast(F32), "%s%d" % (nm, hh)))
    NS_ = len(stage)
    for kc in range(8):
        for (c0, n) in pieces:
            sap, skey = stage[cnt % NS_]
            eng = rot[cnt % len(rot)]
            A0("sp", lambda e, kc=kc, c0=c0, n=n, sap=sap: e.dma_start(
                out=sap[:, 0:n], in_=_dr(w_in, kc * 128, 128, c0, n, DIN)),
               writes=[skey], dma="w" + skey)
            cast_op(eng, Wbf[:, kc, c0:c0 + n], sap[:, 0:n], GT[:, kc:kc + 1], [skey, "GT"], ["Wbf"])
            cnt += 1
    for kc in range(8):
        sap, skey = stage[cnt % NS_]
        eng = rot[cnt % len(rot)]
        A0("sp", lambda e, kc=kc, sap=sap: e.dma_start(out=sap[:, :], in_=_dr(w_out, kc * 128, 128, 0, D, D)),
           writes=[skey], dma="w" + skey)
        cast_op(eng, Wob[:, kc, :], sap[:, :], None, [skey], ["Wob"])
        cnt += 1

    for h in range(8):
        A0("dve", lambda e, h=h: e.tensor_scalar(out=TB01[:, h, 0:128], in0=XB[0][:, h * 128:(h + 1) * 128],
                                                 scalar1=B256[:, h:h + 1], scalar2=8.0, op0=ALU.subtract, op1=ALU.mult),
           reads=["XB0", "B256"], writes=["TB0"])
        A0("dve", lambda e, h=h: e.tensor_scalar(out=TB01[:, h, 128:256], in0=XB[1][:, h * 128:(h + 1) * 128],
                                                 scalar1=B256[:, h:h + 1], scalar2=8.0, op0=ALU.subtract, op1=ALU.mult),
           reads=["XB1", "B256"], writes=["TB1"])
    A0("dve", lambda e: e.memset(TB01[0:64, :, 0:64], MASKV), reads=["TB0"], writes=["TB0"])

    for e_ in ENGS:
        P0.add_dma_barrier(e_)

    P1 = Prog("b")
    A = P1.add
    st = {"xb": 0, "stg": 0, "ktm": 0, "pj": 0, "item": 0, "head": 0}

    def nxt(name, n):
        v = st[name]
        st[name] = (v + 1) % n
        return v

    def store(src_ap, dst_ap, rkeys, okey, slot):
        A("sp", lambda e: e.dma_start(out=dst_ap, in_=src_ap), reads=rkeys, writes=[okey], dma="o_" + slot)
        out_keys.add(okey)

    out_keys = set()

    def ingest_k(src, skey, tp, grp, j, out_dst, defer=False):
        k = nxt("ktm", 2)
        A("dve", lambda e: e.tensor_copy(out=KTM[k][0:tp, :], in_=src), reads=[skey], writes=[("KTM", k)])
        if out_dst is not None:
            s_ = nxt("stg", NSTG)
            if DBG.get('dvecopy'):
                A("dve", lambda e: e.tensor_copy(out=STG[s_][0:tp, :], in_=src), reads=[skey], writes=[("STG", s_)])
            else:
                A("act", lambda e: e.activation(out=STG[s_][0:tp, :], in_=src, func=AF.Copy), reads=[skey],
                  writes=[("STG", s_)])
            if not DBG.get('nodma'):
                store(STG[s_][0:tp, :], out_dst, [("STG", s_)], "k%s" % grp, "stg%d" % s_)

        def stage2():
            ingest_k2(k, tp, grp, j)

        if defer:
            return stage2
        stage2()

    def ingest_k2(k, tp, grp, j):
        half = 0 if grp == "A" else 1
        trk = "TR"
        for c in range(4):
            A("pe", lambda e, c=c: e.transpose(out=TR[:, half * 512 + c * 128: half * 512 + c * 128 + tp],
                                               in_=KTM[k][0:tp, c * 128:(c + 1) * 128], identity=IDb[0:tp, 0:tp]),
              reads=[("KTM", k)], writes=[trk])
        if grp == "A":
            sl = j % 8
            dst = KaT[:, :, sl * 128: sl * 128 + tp]
            dkey = ("KaT", sl)
        else:
            dst = KbT[:, :, j * 128: j * 128 + tp]
            dkey = ("KbT", j)
        srcv = TR[:, half * 512:(half + 1) * 512].rearrange("p (c t) -> p c t", c=4)[:, :, 0:tp]
        A("dve", lambda e: e.tensor_copy(out=dst, in_=srcv), reads=[trk], writes=[dkey])

    def ingest_v(src, skey, tp, grp, j, out_dst):
        if grp == "A":
            sl = j % 8
            VX = VA
            dkey = ("VA", sl)
        else:
            sl = j
            VX = VB
            dkey = ("VB", j)
        srcv = src.rearrange("p (c e d) -> p c e d", c=4, e=2)
        A("dve", lambda e: e.tensor_copy(out=VX[0:tp, sl, :, 0:64], in_=srcv[:, :, 0, :]), reads=[skey], writes=[dkey])
        A("dve", lambda e: e.tensor_copy(out=VX[0:tp, sl, :, 128:192], in_=srcv[:, :, 1, :]), reads=[skey],
          writes=[dkey])
        if out_dst is not None:
            s_ = nxt("stg", NSTG)
            A("act", lambda e: e.activation(out=STG[s_][0:tp, :], in_=src, func=AF.Copy), reads=[skey],
              writes=[("STG", s_)])
            store(STG[s_][0:tp, :], out_dst, [("STG", s_)], "v%s" % grp, "stg%d" % s_)

    def cumsum_blocks(nb, j0, tp):
        A("dve", lambda e: e.tensor_copy(out=LSH[:, 0, :], in_=LACC[:]), reads=["LACC"], writes=["LSH"])
        for b in range(nb):
            if b < nb - 1:
                A("dve", lambda e, b=b: e.tensor_tensor(out=LSH[:, b + 1, :], in0=LSH[:, b, :], in1=LB[:, b, :], op=ALU.add),
                  reads=["LSH", "LB"], writes=["LSH"])
            else:
                A("dve", lambda e, b=b: e.tensor_tensor(out=LACC[:], in0=LSH[:, b, :], in1=LB[:, b, :], op=ALU.add),
                  reads=["LSH", "LB"], writes=["LACC"])
        n = nb * 8
        A("pe", lambda e: e.matmul(MISC[:, 128:128 + n], lhsT=UTRI[:], rhs=LB[:, 0:nb, :].rearrange("p b h -> p (b h)"),
                                   start=True, stop=False), reads=["LB"], writes=["MISC"])
        A("pe", lambda e: e.matmul(MISC[:, 128:128 + n], lhsT=ONES[:], rhs=LSH[:, 0:nb, :].rearrange("p b h -> p (b h)"),
                                   start=False, stop=True), reads=["LSH"], writes=["MISC"])
        A("dve", lambda e: e.tensor_copy(out=CK[:, j0:j0 + nb, :].rearrange("p b h -> p (b h)"), in_=MISC[:, 128:128 + n]),
          reads=["MISC"], writes=["CK"])

    def tile_step(cfg, part, usel=None):
        tp = cfg["tp"]; nsub = cfg["nsub"]; ncols = tp * nsub if nsub > 1 else tp
        iA0 = cfg["iA0"]; iB0 = cfg["iB0"]
        colw = tp

        if DBG['level'] < 1:
            return
        if part in ("p1load", "p5load"):
            u = usel
            xb = nxt("xb", NXB)
            cfg.setdefault("_" + part, {})[u] = xb
            A("sp", lambda e, u=u, xb=xb: e.dma_start(out=XB[xb][0:tp, :], in_=cfg["x_src"](u)), writes=[("XB", xb)],
              dma="x%d" % xb)
            return
        if part == "p1act":
            u = usel
            xb = cfg["_p1load"][u]
            xk = ("XB", xb)
            A("act", lambda e: e.activation(out=JUNK[0:tp, :], in_=XB[xb][0:tp, :], func=AF.Square,
                                            accum_out=SS[0:tp, u:u + 1]), reads=[xk], writes=["JUNK", ("SS", u)])
            A("act", lambda e: e.activation(out=LNT[0:tp, u:u + 1], in_=SS[0:tp, u:u + 1], func=AF.Ln,
                                            bias=EPSC[0:tp, :], scale=1.0 / D), reads=[("SS", u)], writes=[("LNT", u)])
            A("act", lambda e: e.activation(out=RSTD[0:tp, u:u + 1], in_=LNT[0:tp, u:u + 1], func=AF.Exp,
                                            scale=-0.5), reads=[("LNT", u)], writes=[("RSTD", u)])
            return
        if part == "p1pool":
            u = usel
            xb = cfg["_p1load"][u]
            xk = ("XB", xb)
            A("pool", lambda e: e.tensor_scalar(out=HB[0:tp, :], in0=XB[xb][0:tp, :], scalar1=RSTD[0:tp, u:u + 1],
                                                scalar2=1.0, op0=ALU.mult, op1=ALU.mult),
              reads=[xk, ("RSTD", u)], writes=["HB"])
            return
        if part == "p1pe":
            u = usel
            for c in range(8):
                A("pe", lambda e, c=c: e.transpose(out=TR[:, c * 128:c * 128 + tp], in_=HB[0:tp, c * 128:(c + 1) * 128],
                                                   identity=IDb[0:tp, 0:tp]), reads=["HB"], writes=["TR"])
            A("dve", lambda e: e.tensor_copy(out=hT[:, :, u * 128:u * 128 + tp],
                                             in_=TR[:, :].rearrange("p (c t) -> p c t", c=8)[:, :, 0:tp]),
              reads=["TR"], writes=[("hT", u)])
            return

        hkeys = [("hT", u) for u in range(nsub)]
        if DBG['level'] < 2:
            return
        if part == "mid":
            tile_mid(cfg, tp, nsub, ncols, iA0, iB0, colw, hkeys)
        if part not in ("p5a", "p5b") or DBG['level'] < 7:
            return
        if part == "p5b":
            u = usel
            xb = cfg["_p5load"][u]
            xk = ("XB", xb)
            A("dve", lambda e: e.scalar_tensor_tensor(out=XB[xb][0:tp, :], in0=XB[xb][0:tp, :],
                                                      scalar=RSTD[0:tp, 4 + u:5 + u], in1=FG[0:tp, :],
                                                      op0=ALU.mult, op1=ALU.mult),
              reads=[xk, ("RSTD2", u)], writes=[xk])
            store(XB[xb][0:tp, :], cfg["y_dst"](u), [xk], "y", "xb%d" % xb)
            return
        for u in [usel]:
            xb = cfg["_p5load"][u]
            xk = ("XB", xb)
            OB = [PJ[0], PJ[1]] if u % 2 == 0 else [ST[0], ST[1]]
            OK = [("PJ", 0), ("PJ", 1)] if u % 2 == 0 else [("ST", 0), ("ST", 1)]
            for kc in range(8):
                for half in range(2):
                    A("pe", lambda e, kc=kc, half=half, u=u, OB=OB: e.matmul(OB[half][0:tp, :],
                                                                           lhsT=yT[:, kc, u * 128:u * 128 + tp],
                                                                           rhs=Wob[:, kc, half * 512:(half + 1) * 512],
                                                                           start=(kc == 0), stop=(kc == 7)),
                      reads=[("yT", kc)], writes=[OK[half]])
            for half in range(2):
                A("dve", lambda e, half=half, xb=xb, OB=OB: e.tensor_tensor(out=XB[xb][0:tp, half * 512:(half + 1) * 512],
                                                                             in0=OB[half][0:tp, :],
                                                                             in1=XB[xb][0:tp, half * 512:(half + 1) * 512],
                                                                             op=ALU.add),
                  reads=[OK[half], xk], writes=[xk])
            st["pj"] = 0
            A("act", lambda e, u=u, xb=xb: e.activation(out=JUNK[0:tp, :], in_=XB[xb][0:tp, :], func=AF.Square,
                                                        accum_out=SS[0:tp, 4 + u:5 + u]),
              reads=[xk], writes=["JUNK", ("SS2", u)])
            A("act", lambda e, u=u: e.activation(out=LNT[0:tp, 4 + u:5 + u], in_=SS[0:tp, 4 + u:5 + u], func=AF.Ln,
                                                 bias=EPSC[0:tp, :], scale=1.0 / D), reads=[("SS2", u)], writes=[("LNT2", u)])
            A("act", lambda e, u=u: e.activation(out=RSTD[0:tp, 4 + u:5 + u], in_=LNT[0:tp, 4 + u:5 + u], func=AF.Exp,
                                                 scale=-0.5), reads=[("LNT2", u)], writes=[("RSTD2", u)])

    PJL = [PJ[0], PJ[1], ST[0], ST[1]]
    PJK = [("PJ", 0), ("PJ", 1), ("ST", 0), ("ST", 1)]

    def tile_mid(cfg, tp, nsub, ncols, iA0, iB0, colw, hkeys):

        def tokproj(u, c0, n, out_ap, okey):
            for kc in range(8):
                A("pe", lambda e, kc=kc: e.matmul(out_ap, lhsT=hT[:, kc, u * 128:u * 128 + tp],
                                                  rhs=Wbf[:, kc, c0:c0 + n], start=(kc == 0), stop=(kc == 7)),
                  reads=[("hT", u)], writes=[okey])

        for u in range(nsub):
            if not DBG.get('nozf'):
                tokproj(u, ZF0, 8, MISC[0:tp, u * 8:(u + 1) * 8], "MISC")
            jA = iA0 + u
            jB = iB0 + u
            b0 = nxt("pj", 4)
            tokproj(u, KA0, 512, PJL[b0][0:tp, :], PJK[b0])
            fa = ingest_k(PJL[b0][0:tp, :], PJK[b0], tp, "A", jA, cfg["ak_dst"](u), defer=True)
            b1 = nxt("pj", 4)
            tokproj(u, KB0, 512, PJL[b1][0:tp, :], PJK[b1])
            fb = ingest_k(PJL[b1][0:tp, :], PJK[b1], tp, "B", jB, cfg["bk_dst"](u), defer=True)
            b0 = nxt("pj", 4)
            tokproj(u, VA0, 512, PJL[b0][0:tp, :], PJK[b0])
            fa()
            ingest_v(PJL[b0][0:tp, :], PJK[b0], tp, "A", jA, cfg["av_dst"](u))
            b1 = nxt("pj", 4)
            tokproj(u, VB0, 512, PJL[b1][0:tp, :], PJK[b1])
            fb()
            ingest_v(PJL[b1][0:tp, :], PJK[b1], tp, "B", jB, cfg["bv_dst"](u))
            cfg.get("hook_sub", lambda u: None)(u)

        cfg.get("hook_post_p2", lambda: None)()
        if DBG['level'] < 3:
            return
        nz = nsub * 8
        A("dve", lambda e: e.tensor_tensor(out=ZT[0:tp, 0:nz].rearrange("p (u h) -> p u h", u=nsub),
                                           in0=MISC[0:tp, 0:nz].rearrange("p (u h) -> p u h", u=nsub),
                                           in1=_bmid(BFB[0:tp, :], nsub), op=ALU.add),
          reads=["MISC"], writes=["ZT"])
        A("act", lambda e: e.activation(out=ET[0:tp, 0:nz], in_=ZT[0:tp, 0:nz], func=AF.Exp, scale=-1.0),
          reads=["ZT"], writes=["ET"])
        A("act", lambda e: e.activation(out=ZT[0:tp, 0:nz], in_=ET[0:tp, 0:nz], func=AF.Ln, bias=ONES[0:tp, 0:1], scale=1.0),
          reads=["ET"], writes=["ZT"])
        A("dve", lambda e: e.tensor_scalar(out=LB[0:tp, 0:nsub, :].rearrange("p u h -> p (u h)"), in0=ZT[0:tp, 0:nz],
                                           scalar1=-1.0, scalar2=None, op0=ALU.mult), reads=["ZT"], writes=["LB"])
        store(LB[0:tp, 0:nsub, :], cfg["bf_dst"], ["LB"], "bf", "bf")
        def refs_and_bias():
            cumsum_blocks(nsub, iB0, tp)
            sel = SEL64 if tp == 128 else SELS
            A("pe", lambda e: e.matmul(MISC[:, 256:264], lhsT=sel[:], rhs=CK[:, iB0, :], start=True, stop=True),
              reads=["CK"], writes=["MISC"])
            i_last = iB0 + nsub - 1
            A("dve", lambda e: e.tensor_tensor(out=BIAS[:, 0:i_last + 1, :], in0=_bmid(MISC[:, 256:264], i_last + 1),
                                               in1=CK[:, 0:i_last + 1, :], op=ALU.subtract),
              reads=["MISC", "CK"], writes=["BIAS"])
            if nsub > 1:
                ng = 2 * nsub
                for u in range(nsub):
                    A("pe", lambda e, u=u: e.matmul(MISC[0:8, 300 + 2 * u:302 + 2 * u], lhsT=CK[:, iB0 + u, :],
                                                    rhs=E2[:, 0:2], start=True, stop=True), reads=["CK"], writes=["MISC"])
                A("dve", lambda e: e.tensor_scalar(out=TMP8[0:8, 0:ng], in0=MISC[0:8, 300:300 + ng],
                                                   scalar1=MISC[0:8, 300:301], scalar2=8.0, op0=ALU.subtract,
                                                   op1=ALU.mult), reads=["MISC"], writes=["TMP8"])
                A("dve", lambda e: e.tensor_copy(out=DT[0:8, 0:ng * 64].rearrange("p (g t) -> p g t", g=ng),
                                                 in_=bass.AP(TMP8, 0, [[8, 8], [1, ng], [0, 64]])),
                  reads=["TMP8"], writes=["DT"])

        if DBG['level'] < 4:
            return
        for ch in range(8):
            c0 = (GA0 if ch < 4 else GB0) + (ch % 4) * 128
            b = nxt("pj", 4)
            for kc in range(8):
                A("pe", lambda e, kc=kc, c0=c0, b=b: e.matmul(PJL[b][:, 0:ncols], lhsT=Wbf[:, kc, c0:c0 + 128],
                                                               rhs=hT[:, kc, 0:ncols], start=(kc == 0), stop=(kc == 7)),
                  reads=hkeys, writes=[PJK[b]])
            A("act", lambda e, ch=ch, b=b: e.activation(out=SG[:, ch, 0:ncols], in_=PJL[b][:, 0:ncols], func=AF.Silu),
              reads=[PJK[b]], writes=[("SG", ch)])

        if DBG['level'] < 5:
            return
        STL = [ST[0], ST[1], PJ[1], PJ[0]]
        STK = [("ST", 0), ("ST", 1), ("PJ", 1), ("PJ", 0)]

        bgq = []

        def emit_qproj(grp, c, spread=True):
            qs = c % 2
            q0 = (QA0 if grp == "A" else QB0) + c * 128
            b = 0

            TRf = TR[:].bitcast(F32)

            def mm(kcs):
                for kc in kcs:
                    A("pe", lambda e, kc=kc: e.matmul(TRf[:, 0:ncols], lhsT=Wbf[:, kc, q0:q0 + 128],
                                                      rhs=hT[:, kc, 0:ncols], start=(kc == 0), stop=(kc == 7)),
                      reads=hkeys, writes=["TR"])

            def ev():
                A("dve", lambda e: e.tensor_copy(out=QE[qs][0:64, 0:ncols], in_=TRf[0:64, 0:ncols]),
                  reads=["TR"], writes=[("QE", qs)])
                A("dve", lambda e: e.tensor_copy(out=QO[qs][64:128, 0:ncols], in_=TRf[64:128, 0:ncols]),
                  reads=["TR"], writes=[("QO", qs)])

            parts = [lambda: mm([0, 1]), lambda: mm([2, 3]), lambda: mm([4, 5]), lambda: mm([6, 7]), ev]
            if spread:
                bgq.extend(parts)
            else:
                for p_ in parts:
                    p_()

        def attn_group(grp):
            i0 = iA0 if grp == "A" else iB0
            items = []
            for c in range(4):
                for par in range(2):
                    if grp == "A":
                        js = [i0] + list(range(max(0, i0 - 4), i0)) + list(range(i0 + 1, i0 + nsub))
                    else:
                        js = list(range(0, i0 + nsub))
                    for n_, j in enumerate(js):
                        i_lo = max(j, i0)
                        i_hi = min(j + 4, i0 + nsub - 1) if grp == "A" else i0 + nsub - 1
                        items.append(dict(c=c, par=par, h=2 * c + par, j=j, first=(n_ == 0), last=(n_ == len(js) - 1),
                                          i_lo=i_lo, i_hi=i_hi, c0=(i_lo - i0) * colw, c1=(i_hi - i0 + 1) * colw))

            def emit_qk(it):
                c, par, h, j = it["c"], it["par"], it["h"], it["j"]
                qs = c % 2
                if it["first"] and par == 0:
                    while bgq:
                        bgq.pop(0)()
                if it["first"] and par == 1:
                    if c + 1 < 4:
                        emit_qproj(grp, c + 1)
                    elif grp == "A":
                        emit_qproj("B", 0)
                if it["first"]:
                    it_head = st["head"]
                    st["head"] += 1
                    it["ob"] = it_head % 2
                    cur["ob"] = it["ob"]
                else:
                    it["ob"] = cur["ob"]
                idx = st["item"]
                st["item"] += 1
                it["sb"] = idx % 4
                it["pt"] = idx % NPT
                sbk = it["sb"]
                STB = STL[sbk]
                stkey = STK[sbk]
                Q = QE[qs] if par == 0 else QO[qs]
                qk = ("QE", qs) if par == 0 else ("QO", qs)
                c0, c1 = it["c0"], it["c1"]
                if grp == "A":
                    kap = KaT[:, c, (j % 8) * 128:(j % 8) * 128 + 128]
                    kkey = ("KaT", j % 8)
                else:
                    kap = KbT[:, c, j * 128:(j + 1) * 128]
                    kkey = ("KbT", j)
                tabs = []
                for i in range(it["i_lo"], it["i_hi"] + 1):
                    cc0 = (i - i0) * colw
                    if grp == "A":
                        if i == j and i + 1 <= it["i_hi"] and colw == 128:
                            tabs.append((TB01[:, h, 0:256], cc0, 256))
                        elif i == j:
                            tabs.append((TB01[:, h, 0:colw], cc0, colw))
                        if i == j + 1 and not (j >= it["i_lo"] and colw == 128):
                            tabs.append((TB01[:, h, 128:128 + colw], cc0, colw))
                        if i == j + 4:
                            tabs.append((T4[:, 0:colw], cc0, colw))
                    else:
                        if i == j:
                            tabs.append((TDG[:, 0:colw], cc0, colw))
                useD = (grp == "B" and nsub > 1)
                A("pe", lambda e: e.matmul(STB[:, c0:c1], lhsT=kap, rhs=Q[:, c0:c1], start=True,
                                           stop=(len(tabs) == 0 and not useD)),
                  reads=[kkey, qk], writes=[stkey])
                if useD:
                    A("pe", lambda e: e.matmul(STB[:, c0:c1], lhsT=ONESEL[:, h, :], rhs=DT[:, c0:c1], start=False,
                                               stop=(len(tabs) == 0)), reads=["DT"], writes=[stkey])
                for n_, (tap, cc0, tw) in enumerate(tabs):
                    A("pe", lambda e, tap=tap, cc0=cc0, n_=n_, tw=tw: e.matmul(STB[:, cc0:cc0 + tw], lhsT=Jb[:], rhs=tap,
                                                                      start=False, stop=(n_ == len(tabs) - 1)),
                      writes=[stkey])
                pt = it["pt"]
                if grp == "A":
                    A("act", lambda e: e.activation(out=PT[pt][:, c0:c1], in_=STB[:, c0:c1], func=AF.Exp, scale=0.125),
                      reads=[stkey], writes=[("PT", pt)])
                else:
                    A("act", lambda e: e.activation(out=PT[pt][:, c0:c1], in_=STB[:, c0:c1], func=AF.Exp,
                                                    bias=BIAS[:, j, h:h + 1], scale=0.125),
                      reads=[stkey, "BIAS"], writes=[("PT", pt)])

            def emit_pv(it):
                c, par, h, j = it["c"], it["par"], it["h"], it["j"]
                ob, pt = it["ob"], it["pt"]
                c0, c1 = it["c0"], it["c1"]
                if grp == "A":
                    VX, sl, vkey = VA, j % 8, ("VA", j % 8)
                else:
                    VX, sl, vkey = VB, j, ("VB", j)
                if par == 0:
                    lhs = VX[:, sl, c, 0:65]
                    oap = OT[ob][0:65, c0:c1]
                else:
                    lhs = VX[:, sl, c, 64:192]
                    oap = OT[ob][:, c0:c1]
                A("pe", lambda e: e.matmul(oap, lhsT=lhs, rhs=PT[pt][:, c0:c1], start=it["first"], stop=it["last"]),
                  reads=[vkey, ("PT", pt)], writes=[("OT", ob)])
                if it["last"]:
                    ch = (0 if grp == "A" else 4) + c
                    ysl = 0
                    if par == 0:
                        A("dve", lambda e: e.tensor_copy(out=RR[ysl][64:65, 0:ncols], in_=OT[ob][64:65, 0:ncols]),
                          reads=[("OT", ob)], writes=[("RR", ysl)])
                        A("dve", lambda e: e.tensor_tensor(out=Y1[ysl][0:64, 0:ncols], in0=OT[ob][0:64, 0:ncols],
                                                           in1=SG[0:64, ch, 0:ncols], op=ALU.mult),
                          reads=[("OT", ob), ("SG", ch)], writes=[("Y1", ysl)])
                    else:
                        A("dve", lambda e: e.tensor_copy(out=RR[ysl][0:1, 0:ncols], in_=OT[ob][0:1, 0:ncols]),
                          reads=[("OT", ob)], writes=[("RR", ysl)])
                        A("dve", lambda e: e.tensor_tensor(out=Y1[ysl][64:128, 0:ncols], in0=OT[ob][64:128, 0:ncols],
                                                           in1=SG[64:128, ch, 0:ncols], op=ALU.mult),
                          reads=[("OT", ob), ("SG", ch)], writes=[("Y1", ysl)])
                        def fin(ch=ch):
                            for q0 in range(0, ncols, 128):
                                q1 = min(ncols, q0 + 128)
                                A("pe", lambda e, q0=q0, q1=q1: e.matmul(MISC[:, q0:q1], lhsT=SEL2[:], rhs=RR[ysl][:, q0:q1],
                                                                         start=True, stop=True),
                                  reads=[("RR", ysl)], writes=["MISC"])
                            A("act", lambda e: e.activation(out=MISC[:, 0:ncols], in_=MISC[:, 0:ncols], func=AF.Ln),
                              reads=["MISC"], writes=["MISC"])
                            A("act", lambda e: e.activation(out=MISC[:, 0:ncols], in_=MISC[:, 0:ncols], func=AF.Exp,
                                                            scale=-1.0), reads=["MISC"], writes=["MISC"])
                            A("dve", lambda e: e.tensor_tensor(out=yT[:, ch, 0:ncols], in0=Y1[ysl][:, 0:ncols],
                                                               in1=MISC[:, 0:ncols], op=ALU.mult),
                              reads=[("Y1", ysl), "MISC"], writes=[("yT", ch)])

                        pending.append([2, fin])

            cur = {}
            SKEW = 3
            units = list(cfg.get("hook_units", [])) if grp == "B" else []
            idx0 = next((k for k, it in enumerate(items) if it["c"] == 3 and it["par"] == 0), len(items))
            step = max(1, (len(items) - idx0 - 1) // max(1, len(units)))
            pending = []
            for idx in range(len(items) + SKEW):
                if idx < len(items):
                    emit_qk(items[idx])
                for p in list(pending):
                    p[0] -= 1
                    if p[0] <= 0:
                        pending.remove(p)
                        p[1]()
                if idx >= SKEW:
                    emit_pv(items[idx - SKEW])
                if bgq:
                    bgq.pop(0)()
                if units and idx >= idx0 and (idx - idx0) % step == 0:
                    units.pop(0)()
            while units:
                units.pop(0)()
            for p in pending:
                p[1]()

        emit_qproj("A", 0, spread=False)
        refs_and_bias()
        attn_group("A")
        if DBG['level'] < 6:
            return
        attn_group("B")
        while bgq:
            bgq.pop(0)()
        if DBG['level'] < 7:
            return

    cfgs = []
    for s_ in range(nseq):
        for m in range(min(4, DBG['macros'])):
            r0 = s_ * S + m * 512

            def mk(s=s_, m=m, r0=r0):
                last = (m == 3)
                pre = (lambda: A("dve", lambda e: e.memset(LACC[:], 0.0), writes=["LACC"])) if m == 0 else (lambda: None)
                return dict(
                    tp=128, nsub=4, iA0=4 * m, iB0=4 * m, pre=pre,
                    x_src=lambda u: _dr(xp, r0 + u * 128, 128, 0, D, D),
                    y_dst=lambda u: _dr(yp, r0 + u * 128, 128, 0, D, D),
                    ak_dst=(lambda u: _dr(akp, s * 512 + u * 128, 128, 0, 512, 512)) if last else (lambda u: None),
                    av_dst=(lambda u: _dr(avp, s * 512 + u * 128, 128, 0, 512, 512)) if last else (lambda u: None),
                    bk_dst=lambda u: _dr(bkp, r0 + u * 128, 128, 0, 512, 512),
                    bv_dst=lambda u: _dr(bvp, r0 + u * 128, 128, 0, 512, 512),
                    bf_dst=bass.AP(bfp, r0 * 8, [[8, 128], [1024, 4], [1, 8]]),
                )

            cfgs.append(mk())

    def sample_pre():
        cfgs[-1].pop("prev_tail", lambda: None)()

        def load_kv(ksrc, vsrc):
            xb = nxt("xb", NXB)
            A("sp", lambda e: e.dma_start(out=XB[xb][:, 0:512], in_=ksrc), writes=[("XB", xb)], dma="x%d" % xb)
            A("sp", lambda e: e.dma_start(out=XB[xb][:, 512:1024], in_=vsrc), writes=[("XB", xb)], dma="x%d" % xb)
            return xb

        jobs = [("A", j, cak, cav) for j in range(4)] + [("B", j, cbk, cbv) for j in range(16)]
        slots = {}
        AHEAD = 3
        pend2 = None
        for n_ in range(len(jobs) + AHEAD):
            if n_ < len(jobs):
                g_, j_, kt, vt = jobs[n_]
                slots[n_] = load_kv(_dr(kt, j_ * 128, 128, 0, 512, 512), _dr(vt, j_ * 128, 128, 0, 512, 512))
            if n_ >= AHEAD:
                g_, j_, kt, vt = jobs[n_ - AHEAD]
                xb = slots[n_ - AHEAD]
                f2 = ingest_k(XB[xb][:, 0:512], ("XB", xb), 128, g_, j_, None, defer=True)
                if pend2 is not None:
                    pend2()
                pend2 = f2
                ingest_v(XB[xb][:, 512:1024], ("XB", xb), 128, g_, j_, None)
        if pend2 is not None:
            pend2()
        A("dve", lambda e: e.memset(VA[32:64, 4, :, :], 0.0), writes=[("VA", 4)])
        A("dve", lambda e: e.memset(VA[64:128, 4, :, :], 0.0), writes=[("VA", 4)])
        A("dve", lambda e: e.memset(VB[32:64, 16, :, :], 0.0), writes=[("VB", 16)])
        A("dve", lambda e: e.memset(VB[64:128, 16, :, :], 0.0), writes=[("VB", 16)])
        A("dve", lambda e: e.memset(LACC[:], 0.0), writes=["LACC"])
        A("sp", lambda e: e.dma_start(out=LB[:, :, :], in_=bass.AP(cbf, 0, [[8, 128], [1024, 16], [1, 8]])),
          writes=["LB"], dma="clf")
        cumsum_blocks(16, 0, 128)

    if with_sample and DBG['level'] >= 8:
        cfgs.append(dict(
            tp=TS, nsub=1, iA0=4, iB0=16, pre=sample_pre,
            x_src=lambda u: _dr(xs, 0, TS, 0, D, D),
            y_dst=lambda u: _dr(ys, 0, TS, 0, D, D),
            ak_dst=lambda u: _dr(aks, 0, TS, 0, 512, 512),
            av_dst=lambda u: _dr(avs, 0, TS, 0, 512, 512),
            bk_dst=lambda u: _dr(bks, 0, TS, 0, 512, 512),
            bv_dst=lambda u: _dr(bvs, 0, TS, 0, 512, 512),
            bf_dst=bass.AP(bfs, 0, [[8, TS], [8 * TS, 1], [1, 8]]),
        ))

    c0_ = cfgs[0]
    for u in range(c0_["nsub"]):
        tile_step(c0_, "p1load", u)
    for u in range(c0_["nsub"]):
        tile_step(c0_, "p1act", u)
    for u in range(c0_["nsub"]):
        tile_step(c0_, "p1pool", u)
        tile_step(c0_, "p1pe", u)
    for n, cfg in enumerate(cfgs):
        nx = cfgs[n + 1] if n + 1 < len(cfgs) else None

        def hook_sub(u, cfg=cfg, nx=nx):
            if u == 0:
                cfg.pop("prev_tail", lambda: None)()
                if nx is not None:
                    for v in range(nx["nsub"]):
                        tile_step(nx, "p1load", v)
            elif nx is not None and u - 1 < nx["nsub"]:
                tile_step(nx, "p1act", u - 1)
                nx.setdefault("_acted", set()).add(u - 1)

        def hook_post(cfg=cfg, nx=nx):
            if nx is not None:
                for v in range(nx["nsub"]):
                    if v not in nx.get("_acted", set()):
                        tile_step(nx, "p1act", v)
                        nx.setdefault("_acted", set()).add(v)

        cfg["hook_sub"] = hook_sub
        cfg["hook_post_p2"] = hook_post
        if nx is not None:
            def mk_unit(t, nx=nx, cfg=cfg):
                def unit():
                    if t == 0:
                        tile_step(nx, "p1pool", 0)
                        return
                    u = t - 1
                    tile_step(nx, "p1pe", u)
                    if u + 1 < nx["nsub"]:
                        tile_step(nx, "p1pool", u + 1)
                    if u < cfg["nsub"]:
                        tile_step(cfg, "p5load", u)
                return unit

            cfg["hook_units"] = [mk_unit(t) for t in range(nx["nsub"] + 1)]
        cfg["pre"]()
        tile_step(cfg, "mid")
        for u in range(cfg["nsub"]):
            if u not in cfg.get("_p5load", {}):
                tile_step(cfg, "p5load", u)
        for u in range(cfg["nsub"]):
            tile_step(cfg, "p5a", u)
            if u >= 1:
                tile_step(cfg, "p5b", u - 1)

        def tail(cfg=cfg):
            tile_step(cfg, "p5b", cfg["nsub"] - 1)

        if nx is None:
            tail()
        else:
            nx["prev_tail"] = tail

    A("sp", lambda e: None, reads=sorted(out_keys))

    P0.finalize()
    P1.finalize()
    sems = {}
    for n in P0.sem_names() + P1.sem_names():
        sems[n] = es.enter_context(nc.semaphore(n))
    with nc.allow_low_precision("bf16 operands by design"):
        with nc.Block() as block:
            P0.emit(block, sems)
        with nc.Block() as block:
            P1.emit(block, sems)
    es.close()
    return nc


_CACHE = {}


def _get_program(nseq):
    if nseq not in _CACHE:
        _CACHE[nseq] = build_program(nseq)
    return _CACHE[nseq]


def make_in_maps(inputs, nseq, ncores):
    f = lambda a: np.ascontiguousarray(np.asarray(a, dtype=np.float32))
    x_prompt = f(inputs["x_prompt"]); x_sample = f(inputs["x_sample"])
    in_maps = []
    for c in range(ncores):
        in_maps.append({
            "xp": x_prompt[c * nseq:(c + 1) * nseq].reshape(nseq * S, D),
            "xs": x_sample[c],
            "cak": f(inputs["cache_a_k"])[0, c].reshape(512, 512),
            "cav": f(inputs["cache_a_v"])[0, c].reshape(512, 512),
            "cbk": f(inputs["cache_b_k"])[0, c].reshape(S, 512),
            "cbv": f(inputs["cache_b_v"])[0, c].reshape(S, 512),
            "cbf": f(inputs["cache_b_logf"])[0, c].reshape(S, 8),
            "gain": f(inputs["norm_gain"]).reshape(1, D),
            "w_in": f(inputs["w_in"])[0],
            "bfg": f(inputs["b_forget"]).reshape(1, 8),
            "rb": f(inputs["rel_bias"])[0],
            "w_out": f(inputs["w_out"])[0],
            "fgn": f(inputs["final_gain"]).reshape(1, D),
        })
    return in_maps


def kernel(**inputs):
    nseq = NSEQ_CORE
    nc = _get_program(nseq)
    in_maps = make_in_maps(inputs, nseq, NCORES)
    res = run_bass_kernel_spmd(nc, in_maps, core_ids=list(range(NCORES)))
    R = res.results
    cat = lambda k: np.concatenate([r[k][None] if False else r[k] for r in R], axis=0)
    B = NCORES * nseq
    y_prompt = cat("yp").reshape(B, S, D)
    y_sample = np.stack([r["ys"] for r in R], 0)
    akp = cat("akp").reshape(1, B, 512, 8, 64)
    avp = cat("avp").reshape(1, B, 512, 8, 64)
    bkp = cat("bkp").reshape(1, B, S, 8, 64)
    bvp = cat("bvp").reshape(1, B, S, 8, 64)
    bfp = cat("bfp").reshape(1, B, S, 8)
    aks = np.stack([r["aks"] for r in R], 0).reshape(1, NCORES, TS, 8, 64)
    avs = np.stack([r["avs"] for r in R], 0).reshape(1, NCORES, TS, 8, 64)
    bks = np.stack([r["bks"] for r in R], 0).reshape(1, NCORES, TS, 8, 64)
    bvs = np.stack([r["bvs"] for r in R], 0).reshape(1, NCORES, TS, 8, 64)
    bfs = np.stack([r["bfs"] for r in R], 0).reshape(1, NCORES, TS, 8)
    return (y_prompt, y_sample, akp, avp, bkp, bvp, bfp, aks, avs, bks, bvs, bfs)
```

```python
import contextlib
import numpy as np
import concourse.bass as bass
import concourse.mybir as mybir
from concourse.bass_utils import run_bass_kernel_spmd
from concourse.alu_op_type import AluOpType as ALU

F32 = mybir.dt.float32
BF16 = mybir.dt.bfloat16
AF = mybir.ActivationFunctionType

D = 1024
S = 2048
DIN = 4104
NCORES = 8
NSEQ_CORE = 4
TS = 32
QA0, KA0, VA0, GA0, QB0, KB0, VB0, GB0, ZF0 = 0, 512, 1024, 1536, 2048, 2560, 3072, 3584, 4096
MASKV = -30000.0
EPS = 1e-6

ENGS = ("pe", "act", "dve", "pool", "sp")


class Op:
    __slots__ = ("eng", "fn", "deps", "signal", "sem", "val", "is_dma", "waits")

    def __init__(self, eng, fn):
        self.eng = eng
        self.fn = fn
        self.deps = []
        self.signal = False
        self.sem = None
        self.val = 0
        self.is_dma = False
        self.waits = []


def _is_psum_key(b):
    if isinstance(b, tuple):
        return b[0] in ("PJ", "ST", "OT")
    return b in ("TR", "MISC")


class Prog:
    def __init__(self, prefix):
        self.prefix = prefix
        self.q = {e: [] for e in ENGS}
        self.writers = {}
        self.readers = {}
        self.dma_cnt = {}

    def add(self, eng, fn, reads=(), writes=(), dma=None):
        op = Op(eng, fn)
        if dma is not None:
            op.is_dma = True
            op.sem = self.prefix + "d_" + dma
            self.dma_cnt[op.sem] = self.dma_cnt.get(op.sem, 0) + 16
            op.val = self.dma_cnt[op.sem]
            op.signal = True
        else:
            op.sem = self.prefix + eng
        excl = [b for b in reads if _is_psum_key(b)]
        if excl:
            reads = [b for b in reads if not _is_psum_key(b)]
            writes = list(writes) + [b for b in excl if b not in writes]
        deps = {}
        for b in reads:
            for d in self.writers.get(b, {}).values():
                deps[id(d)] = d
        for b in writes:
            for d in self.writers.get(b, {}).values():
                deps[id(d)] = d
            for d in self.readers.get(b, {}).values():
                deps[id(d)] = d
        for d in deps.values():
            if eng == "pe" and (not op.is_dma) and d.sem == op.sem:
                continue
            op.deps.append(d)
            d.signal = True
        for b in reads:
            self.readers.setdefault(b, {})[op.sem] = op
        for b in writes:
            self.writers.setdefault(b, {})[op.sem] = op
        self.q[eng].append(op)
        return op

    def add_dma_barrier(self, eng):
        op = Op(eng, lambda e: None)
        op.sem = self.prefix + eng
        last = {}
        for e_ in ENGS:
            for o in self.q[e_]:
                if o.is_dma:
                    last[o.sem] = o
        op.deps = list(last.values())
        self.q[eng].append(op)
        return op

    def finalize(self):
        for e in ENGS:
            c = 0
            for op in self.q[e]:
                if op.is_dma:
                    continue
                if op.signal:
                    c += 1
                    op.val = c
        for e in ENGS:
            waited = {}
            for op in self.q[e]:
                need = {}
                for d in op.deps:
                    if need.get(d.sem, 0) < d.val:
                        need[d.sem] = d.val
                op.waits = []
                for s, v in need.items():
                    if waited.get(s, 0) < v:
                        waited[s] = v
                        op.waits.append((s, v))

    def sem_names(self):
        names = set(self.prefix + e for e in ENGS)
        names.update(self.dma_cnt.keys())
        return sorted(names)

    def emit(self, block, sems):
        decos = {"pe": block.tensor, "act": block.scalar, "dve": block.vector,
                 "pool": block.gpsimd, "sp": block.sync}
        for e in ENGS:
            ops = self.q[e]
            if not ops:
                continue

            def body(eng, ops=ops):
                for op in ops:
                    for s, v in op.waits:
                        eng.wait_ge(sems[s], v)
                    inst = op.fn(eng)
                    if op.signal:
                        inst.then_inc(sems[op.sem], 16 if op.is_dma else 1)

            decos[e](body)


def _dr(t, row0, nrows, col0, ncols, rowlen):
    return bass.AP(t, row0 * rowlen + col0, [[rowlen, nrows], [1, ncols]])


def _bmid(ap, n):
    a = [list(x) for x in ap.ap]
    return bass.AP(ap.tensor, ap.offset, [a[0], [0, n]] + a[1:])


DBG = {'level': 99, 'macros': 99}


def build_program(nseq, with_sample=True):
    nc = bass.Bass("TRN2", target_bir_lowering=False, dynamic_dma_scratch_size=256)
    es = contextlib.ExitStack()

    def din(name, shape):
        return nc.dram_tensor(name, shape, F32, kind="ExternalInput")

    def dout(name, shape):
        return nc.dram_tensor(name, shape, F32, kind="ExternalOutput")

    xp = din("xp", [nseq * S, D])
    xs = din("xs", [TS, D])
    cak = din("cak", [512, 512]); cav = din("cav", [512, 512])
    cbk = din("cbk", [S, 512]); cbv = din("cbv", [S, 512]); cbf = din("cbf", [S, 8])
    gain = din("gain", [1, D]); w_in = din("w_in", [D, DIN]); bfg = din("bfg", [1, 8])
    rb = din("rb", [8, 257]); w_out = din("w_out", [D, D]); fgn = din("fgn", [1, D])
    yp = dout("yp", [nseq * S, D]); ys = dout("ys", [TS, D])
    akp = dout("akp", [nseq * 512, 512]); avp = dout("avp", [nseq * 512, 512])
    bkp = dout("bkp", [nseq * S, 512]); bvp = dout("bvp", [nseq * S, 512]); bfp = dout("bfp", [nseq * S, 8])
    aks = dout("aks", [TS, 512]); avs = dout("avs", [TS, 512])
    bks = dout("bks", [TS, 512]); bvs = dout("bvs", [TS, 512]); bfs = dout("bfs", [TS, 8])
    rbpad = nc.dram_tensor("rbpad", [8, 392], F32, kind="Internal")

    def sb(name, shape, dt):
        return es.enter_context(nc.sbuf_tensor(name, shape, dt))

    def ps(name, shape, dt):
        return es.enter_context(nc.psum_tensor(name, shape, dt))

    Wbf = sb("Wbf", [128, 8, DIN], BF16)
    Wob = sb("Wob", [128, 8, D], BF16)
    NXB = 4
    XB = [sb("XB%d" % i, [128, D], F32) for i in range(NXB)]
    HB = sb("HB", [128, D], BF16)
    JUNK = sb("JUNK", [128, D], BF16)
    hT = sb("hT", [128, 8, 512], BF16)
    KaT = sb("KaT", [128, 4, 8 * 128], BF16)
    KbT = sb("KbT", [128, 4, 17 * 128], BF16)
    VA = sb("VA", [128, 8, 4, 192], BF16)
    VB = sb("VB", [128, 17, 4, 192], BF16)
    QE = [sb("QE%d" % i, [128, 512], BF16) for i in range(2)]
    QO = [sb("QO%d" % i, [128, 512], BF16) for i in range(2)]
    SG = sb("SG", [128, 8, 512], BF16)
    NSTG = 2
    STG = [sb("STG%d" % i, [128, 512], F32) for i in range(NSTG)]
    KTM = [sb("KTM%d" % i, [128, 512], BF16) for i in range(2)]
    NPT = 4
    PT = [sb("PT%d" % i, [128, 512], BF16) for i in range(NPT)]
    Y1 = [sb("Y1%d" % i, [128, 512], BF16) for i in range(1)]
    RR = [sb("RR%d" % i, [128, 512], F32) for i in range(1)]
    yT = sb("yT", [128, 8, 512], BF16)
    FG = sb("FG", [128, D], F32)
    TB01 = sb("TB01", [128, 8, 256], BF16)
    T4 = sb("T4", [128, 128], BF16)
    TDG = sb("TDG", [128, 128], BF16)
    Jb = sb("Jb", [128, 128], BF16)
    IDb = sb("IDb", [128, 128], BF16)
    ONES = sb("ONES", [128, 128], F32)
    UTRI = sb("UTRI", [128, 128], F32)
    SEL64 = sb("SEL64", [128, 128], F32)
    SELS = sb("SELS", [128, 128], F32)
    SEL2 = sb("SEL2", [128, 128], F32)
    M30 = sb("M30", [128, 128], F32)
    GT = sb("GT", [128, 8], F32)
    BFB = sb("BFB", [128, 8], F32)
    B256 = sb("B256", [128, 8], F32)
    EPSC = sb("EPSC", [128, 1], F32)
    RBP = sb("RBP", [8, 392], F32)
    SS = sb("SS", [128, 8], F32)
    LNT = sb("LNT", [128, 8], F32)
    RSTD = sb("RSTD", [128, 8], F32)
    ZT = sb("ZT", [128, 32], F32)
    ET = sb("ET", [128, 32], F32)
    LB = sb("LB", [128, 16, 8], F32)
    LSH = sb("LSH", [128, 17, 8], F32)
    LACC = sb("LACC", [128, 8], F32)
    CK = sb("CK", [128, 17, 8], F32)
    BIAS = sb("BIAS", [128, 17, 8], F32)
    ONESEL = sb("ONESEL", [128, 8, 128], BF16)
    DT = sb("DT", [128, 512], BF16)
    TMP8 = sb("TMP8", [128, 8], F32)
    E2 = sb("E2", [128, 2], F32)

    PJ = [ps("PJ%d" % i, [128, 512], F32) for i in range(2)]
    ST = [ps("ST%d" % i, [128, 512], F32) for i in range(2)]
    OT = [ps("OT%d" % i, [128, 512], F32) for i in range(2)]
    TR = ps("TR", [128, 1024], BF16)
    MISC = ps("MISC", [128, 512], F32)

    P0 = Prog("a")
    A0 = P0.add

    A0("pool", lambda e: e.memset(ONES[:], 1.0), writes=["ONES"])
    A0("pool", lambda e: e.memset(M30[:], MASKV), writes=["M30"])
    A0("pool", lambda e: e.memset(EPSC[:], EPS), writes=["EPSC"])
    A0("pool", lambda e: e.affine_select(out=IDb[:], in_=ONES[:], pattern=[[-1, 128]], compare_op=ALU.is_equal,
                                         fill=0.0, base=0, channel_multiplier=1), reads=["ONES"], writes=["IDb"])
    A0("pool", lambda e: e.affine_select(out=Jb[:], in_=ONES[:], pattern=[[1, 128]], compare_op=ALU.is_equal,
                                         fill=0.0, base=-127, channel_multiplier=1), reads=["ONES"], writes=["Jb"])
    A0("pool", lambda e: e.affine_select(out=UTRI[:], in_=ONES[:], pattern=[[1, 128]], compare_op=ALU.is_ge,
                                         fill=0.0, base=0, channel_multiplier=-1), reads=["ONES"], writes=["UTRI"])
    A0("pool", lambda e: e.affine_select(out=TDG[:], in_=M30[:], pattern=[[-1, 128]], compare_op=ALU.is_gt,
                                         fill=0.0, base=127, channel_multiplier=-1), reads=["M30"], writes=["TDG"])
    A0("pool", lambda e: e.memset(T4[:], 0.0), writes=["T4"])
    A0("pool", lambda e: e.memset(T4[64:128, 64:128], MASKV), writes=["T4"])
    A0("pool", lambda e: e.memset(SEL64[:], 0.0), writes=["SEL64"])
    A0("pool", lambda e: e.memset(SEL64[32:33, :], 1.0), writes=["SEL64"])
    A0("pool", lambda e: e.memset(E2[:], 0.0), writes=["E2"])
    A0("pool", lambda e: e.memset(E2[32:33, 0:1], 1.0), writes=["E2"])
    A0("pool", lambda e: e.memset(E2[96:97, 1:2], 1.0), writes=["E2"])
    A0("pool", lambda e: e.memset(SELS[:], 0.0), writes=["SELS"])
    A0("pool", lambda e: e.memset(SELS[0:1, :], 1.0), writes=["SELS"])
    A0("pool", lambda e: e.memset(SEL2[:], 0.0), writes=["SEL2"])
    A0("pool", lambda e: e.memset(SEL2[64:65, 0:64], 1.0), writes=["SEL2"])
    A0("pool", lambda e: e.memset(SEL2[0:1, 64:128], 1.0), writes=["SEL2"])
    A0("pool", lambda e: e.memset(KaT[:], 0.0), writes=["KaT"])
    A0("pool", lambda e: e.memset(KbT[:], 0.0), writes=["KbT"])
    A0("pool", lambda e: e.memset(VA[:], 0.0), writes=["VA"])
    A0("pool", lambda e: e.memset(VB[:], 0.0), writes=["VB"])
    A0("pool", lambda e: e.memset(VA[:, :, :, 64:65], 1.0), writes=["VA"])
    A0("pool", lambda e: e.memset(VB[:, :, :, 64:65], 1.0), writes=["VB"])
    for i in range(2):
        A0("pool", lambda e, i=i: e.memset(QE[i][:], 0.0), writes=["QE"])
        A0("pool", lambda e, i=i: e.memset(QO[i][:], 0.0), writes=["QO"])
        if i < len(RR):
            A0("pool", lambda e, i=i: e.memset(RR[i][:], 0.0), writes=["RR"])
    A0("pool", lambda e: e.memset(LB[:], 0.0), writes=["LB"])
    A0("pool", lambda e: e.memset(LSH[:], 0.0), writes=["LSH"])
    A0("pool", lambda e: e.memset(CK[:], 0.0), writes=["CK"])
    A0("pool", lambda e: e.memset(BIAS[:], 0.0), writes=["BIAS"])
    A0("pool", lambda e: e.memset(DT[:], 0.0), writes=["DT"])
    A0("pool", lambda e: e.memset(TMP8[:], 0.0), writes=["TMP8"])
    A0("pool", lambda e: e.affine_select(out=ONESEL[:], in_=bass.AP(ONES, 0, [[128, 128], [0, 8], [1, 128]]),
                                         pattern=[[-1, 8], [0, 128]], compare_op=ALU.is_equal, fill=0.0, base=0,
                                         channel_multiplier=1), reads=["ONES"], writes=["ONESEL"])

    for kc in range(8):
        A0("sp", lambda e, kc=kc: e.dma_start(out=GT[:, kc:kc + 1], in_=bass.AP(gain, kc * 128, [[1, 128], [1, 1]])),
           writes=["GT"], dma="gt")
    for h in range(8):
        A0("act", lambda e, h=h: e.dma_start(out=B256[:, h:h + 1], in_=bass.AP(rb, h * 257 + 256, [[0, 128], [1, 1]])),
           writes=["B256"], dma="b256")
    A0("act", lambda e: e.dma_start(out=BFB[:], in_=bass.AP(bfg, 0, [[0, 128], [1, 8]])), writes=["BFB"], dma="bfb")
    A0("act", lambda e: e.dma_start(out=FG[:], in_=bass.AP(fgn, 0, [[0, 128], [1, D]])), writes=["FG"], dma="fg")
    A0("act", lambda e: e.dma_start(out=RBP[:, 0:257], in_=rb.ap()), writes=["RBP"], dma="rbp0")
    A0("act", lambda e: e.activation(out=RBP[:, 257:392],
                                     in_=bass.AP(RBP[:, 256:257].tensor, RBP[:, 256:257].offset,
                                                 [list(RBP[:, 256:257].ap[0]), [0, 135]]), func=AF.Copy),
       reads=["RBP"], writes=["RBP2"])
    A0("act", lambda e: e.dma_start(out=rbpad.ap(), in_=RBP[:]), reads=["RBP", "RBP2"], writes=["rbpad"], dma="rbp")
    for h in range(8):
        A0("act", lambda e, h=h: e.dma_start(out=XB[0][:, h * 128:(h + 1) * 128],
                                            in_=bass.AP(rbpad, h * 392 + 1, [[1, 128], [1, 128]])),
           reads=["rbpad"], writes=["XB0"], dma="tb0")
        A0("act", lambda e, h=h: e.dma_start(out=XB[1][:, h * 128:(h + 1) * 128],
                                            in_=bass.AP(rbpad, h * 392 + 129, [[1, 128], [1, 128]])),
           reads=["rbpad"], writes=["XB1"], dma="tb1")
    pieces = [(0, 1024), (1024, 1024), (2048, 1024), (3072, 1024), (4096, 8)]
    cnt = 0
    rot = ["dve", "act", "dve", "act", "pool", "dve", "act"]

    def cast_op(eng, out_ap, in_ap, scale_ap, rk, wk):
        if eng == "act":
            if scale_ap is None:
                A0("act", lambda e: e.activation(out=out_ap, in_=in_ap, func=AF.Copy), reads=rk, writes=wk)
            else:
                A0("act", lambda e: e.activation(out=out_ap, in_=in_ap, func=AF.Copy, scale=scale_ap), reads=rk, writes=wk)
        elif scale_ap is None:
            A0(eng, lambda e: e.tensor_copy(out=out_ap, in_=in_ap), reads=rk, writes=wk)
        else:
            A0(eng, lambda e: e.tensor_scalar(out=out_ap, in0=in_ap, scalar1=scale_ap, scalar2=1.0, op0=ALU.mult,
                                              op1=ALU.mult), reads=rk, writes=wk)

    stage = [(XB[i][:, :], "XB%d" % i) for i in (2, 3)]
    for nm, t in (("hT", hT), ("yT", yT), ("SG", SG)):
        for hh in range(2):
            stage.append((t[:, 4 * hh:4 * hh + 4, :].rearrange("p a b -> p (a b)").bitcast(F32), "%s%d" % (nm, hh)))
    NS_ = len(stage)
    for kc in range(8):
        for (c0, n) in pieces:
            sap, skey = stage[cnt % NS_]
            eng = rot[cnt % len(rot)]
            A0("sp", lambda e, kc=kc, c0=c0, n=n, sap=sap: e.dma_start(
                out=sap[:, 0:n], in_=_dr(w_in, kc * 128, 128, c0, n, DIN)),
               writes=[skey], dma="w" + skey)
            cast_op(eng, Wbf[:, kc, c0:c0 + n], sap[:, 0:n], GT[:, kc:kc + 1], [skey, "GT"], ["Wbf"])
            cnt += 1
    for kc in range(8):
        sap, skey = stage[cnt % NS_]
        eng = rot[cnt % len(rot)]
        A0("sp", lambda e, kc=kc, sap=sap: e.dma_start(out=sap[:, :], in_=_dr(w_out, kc * 128, 128, 0, D, D)),
           writes=[skey], dma="w" + skey)
        cast_op(eng, Wob[:, kc, :], sap[:, :], None, [skey], ["Wob"])
        cnt += 1

    for h in range(8):
        A0("dve", lambda e, h=h: e.tensor_scalar(out=TB01[:, h, 0:128], in0=XB[0][:, h * 128:(h + 1) * 128],
                                                 scalar1=B256[:, h:h + 1], scalar2=8.0, op0=ALU.subtract, op1=ALU.mult),
           reads=["XB0", "B256"], writes=["TB0"])
        A0("dve", lambda e, h=h: e.tensor_scalar(out=TB01[:, h, 128:256], in0=XB[1][:, h * 128:(h + 1) * 128],
                                                 scalar1=B256[:, h:h + 1], scalar2=8.0, op0=ALU.subtract, op1=ALU.mult),
           reads=["XB1", "B256"], writes=["TB1"])
    A0("dve", lambda e: e.memset(TB01[0:64, :, 0:64], MASKV), reads=["TB0"], writes=["TB0"])

    for e_ in ENGS:
        P0.add_dma_barrier(e_)

    P1 = Prog("b")
    A = P1.add
    st = {"xb": 0, "stg": 0, "ktm": 0, "pj": 0, "item": 0, "head": 0}

    def nxt(name, n):
        v = st[name]
        st[name] = (v + 1) % n
        return v

    def store(src_ap, dst_ap, rkeys, okey, slot):
        A("sp", lambda e: e.dma_start(out=dst_ap, in_=src_ap), reads=rkeys, writes=[okey], dma="o_" + slot)
        out_keys.add(okey)

    out_keys = set()

    def ingest_k(src, skey, tp, grp, j, out_dst, defer=False):
        k = nxt("ktm", 2)
        A("dve", lambda e: e.tensor_copy(out=KTM[k][0:tp, :], in_=src), reads=[skey], writes=[("KTM", k)])
        if out_dst is not None:
            s_ = nxt("stg", NSTG)
            if DBG.get('dvecopy'):
                A("dve", lambda e: e.tensor_copy(out=STG[s_][0:tp, :], in_=src), reads=[skey], writes=[("STG", s_)])
            else:
                A("act", lambda e: e.activation(out=STG[s_][0:tp, :], in_=src, func=AF.Copy), reads=[skey],
                  writes=[("STG", s_)])
            if not DBG.get('nodma'):
                store(STG[s_][0:tp, :], out_dst, [("STG", s_)], "k%s" % grp, "stg%d" % s_)

        def stage2():
            ingest_k2(k, tp, grp, j)

        if defer:
            return stage2
        stage2()

    def ingest_k2(k, tp, grp, j):
        half = 0 if grp == "A" else 1
        trk = "TR"
        for c in range(4):
            A("pe", lambda e, c=c: e.transpose(out=TR[:, half * 512 + c * 128: half * 512 + c * 128 + tp],
                                               in_=KTM[k][0:tp, c * 128:(c + 1) * 128], identity=IDb[0:tp, 0:tp]),
              reads=[("KTM", k)], writes=[trk])
        if grp == "A":
            sl = j % 8
            dst = KaT[:, :, sl * 128: sl * 128 + tp]
            dkey = ("KaT", sl)
        else:
            dst = KbT[:, :, j * 128: j * 128 + tp]
            dkey = ("KbT", j)
        srcv = TR[:, half * 512:(half + 1) * 512].rearrange("p (c t) -> p c t", c=4)[:, :, 0:tp]
        A("dve", lambda e: e.tensor_copy(out=dst, in_=srcv), reads=[trk], writes=[dkey])

    def ingest_v(src, skey, tp, grp, j, out_dst):
        if grp == "A":
            sl = j % 8
            VX = VA
            dkey = ("VA", sl)
        else:
            sl = j
            VX = VB
            dkey = ("VB", j)
        srcv = src.rearrange("p (c e d) -> p c e d", c=4, e=2)
        A("dve", lambda e: e.tensor_copy(out=VX[0:tp, sl, :, 0:64], in_=srcv[:, :, 0, :]), reads=[skey], writes=[dkey])
        A("dve", lambda e: e.tensor_copy(out=VX[0:tp, sl, :, 128:192], in_=srcv[:, :, 1, :]), reads=[skey],
          writes=[dkey])
        if out_dst is not None:
            s_ = nxt("stg", NSTG)
            A("act", lambda e: e.activation(out=STG[s_][0:tp, :], in_=src, func=AF.Copy), reads=[skey],
              writes=[("STG", s_)])
            store(STG[s_][0:tp, :], out_dst, [("STG", s_)], "v%s" % grp, "stg%d" % s_)

    def cumsum_blocks(nb, j0, tp):
        A("dve", lambda e: e.tensor_copy(out=LSH[:, 0, :], in_=LACC[:]), reads=["LACC"], writes=["LSH"])
        for b in range(nb):
            if b < nb - 1:
                A("dve", lambda e, b=b: e.tensor_tensor(out=LSH[:, b + 1, :], in0=LSH[:, b, :], in1=LB[:, b, :], op=ALU.add),
                  reads=["LSH", "LB"], writes=["LSH"])
            else:
                A("dve", lambda e, b=b: e.tensor_tensor(out=LACC[:], in0=LSH[:, b, :], in1=LB[:, b, :], op=ALU.add),
                  reads=["LSH", "LB"], writes=["LACC"])
        n = nb * 8
        A("pe", lambda e: e.matmul(MISC[:, 128:128 + n], lhsT=UTRI[:], rhs=LB[:, 0:nb, :].rearrange("p b h -> p (b h)"),
                                   start=True, stop=False), reads=["LB"], writes=["MISC"])
        A("pe", lambda e: e.matmul(MISC[:, 128:128 + n], lhsT=ONES[:], rhs=LSH[:, 0:nb, :].rearrange("p b h -> p (b h)"),
                                   start=False, stop=True), reads=["LSH"], writes=["MISC"])
        A("dve", lambda e: e.tensor_copy(out=CK[:, j0:j0 + nb, :].rearrange("p b h -> p (b h)"), in_=MISC[:, 128:128 + n]),
          reads=["MISC"], writes=["CK"])

    def tile_step(cfg, part, usel=None):
        tp = cfg["tp"]; nsub = cfg["nsub"]; ncols = tp * nsub if nsub > 1 else tp
        iA0 = cfg["iA0"]; iB0 = cfg["iB0"]
        colw = tp

        if DBG['level'] < 1:
            return
        if part in ("p1load", "p5load"):
            u = usel
            xb = nxt("xb", NXB)
            cfg.setdefault("_" + part, {})[u] = xb
            A("sp", lambda e, u=u, xb=xb: e.dma_start(out=XB[xb][0:tp, :], in_=cfg["x_src"](u)), writes=[("XB", xb)],
              dma="x%d" % xb)
            return
        if part == "p1act":
            u = usel
            xb = cfg["_p1load"][u]
            xk = ("XB", xb)
            A("act", lambda e: e.activation(out=JUNK[0:tp, :], in_=XB[xb][0:tp, :], func=AF.Square,
                                            accum_out=SS[0:tp, u:u + 1]), reads=[xk], writes=["JUNK", ("SS", u)])
            A("act", lambda e: e.activation(out=LNT[0:tp, u:u + 1], in_=SS[0:tp, u:u + 1], func=AF.Ln,
                                            bias=EPSC[0:tp, :], scale=1.0 / D), reads=[("SS", u)], writes=[("LNT", u)])
            A("act", lambda e: e.activation(out=RSTD[0:tp, u:u + 1], in_=LNT[0:tp, u:u + 1], func=AF.Exp,
                                            scale=-0.5), reads=[("LNT", u)], writes=[("RSTD", u)])
            return
        if part == "p1pool":
            u = usel
            xb = cfg["_p1load"][u]
            xk = ("XB", xb)
            A("pool", lambda e: e.tensor_scalar(out=HB[0:tp, :], in0=XB[xb][0:tp, :], scalar1=RSTD[0:tp, u:u + 1],
                                                scalar2=1.0, op0=ALU.mult, op1=ALU.mult),
              reads=[xk, ("RSTD", u)], writes=["HB"])
            return
        if part == "p1pe":
            u = usel
            for c in range(8):
                A("pe", lambda e, c=c: e.transpose(out=TR[:, c * 128:c * 128 + tp], in_=HB[0:tp, c * 128:(c + 1) * 128],
                                                   identity=IDb[0:tp, 0:tp]), reads=["HB"], writes=["TR"])
            A("dve", lambda e: e.tensor_copy(out=hT[:, :, u * 128:u * 128 + tp],
                                             in_=TR[:, :].rearrange("p (c t) -> p c t", c=8)[:, :, 0:tp]),
              reads=["TR"], writes=[("hT", u)])
            return

        hkeys = [("hT", u) for u in range(nsub)]
        if DBG['level'] < 2:
            return
        if part == "mid":
            tile_mid(cfg, tp, nsub, ncols, iA0, iB0, colw, hkeys)
        if part not in ("p5a", "p5b") or DBG['level'] < 7:
            return
        if part == "p5b":
            u = usel
            xb = cfg["_p5load"][u]
            xk = ("XB", xb)
            A("dve", lambda e: e.scalar_tensor_tensor(out=XB[xb][0:tp, :], in0=XB[xb][0:tp, :],
                                                      scalar=RSTD[0:tp, 4 + u:5 + u], in1=FG[0:tp, :],
                                                      op0=ALU.mult, op1=ALU.mult),
              reads=[xk, ("RSTD2", u)], writes=[xk])
            store(XB[xb][0:tp, :], cfg["y_dst"](u), [xk], "y", "xb%d" % xb)
            return
        for u in [usel]:
            xb = cfg["_p5load"][u]
            xk = ("XB", xb)
            OB = [PJ[0], PJ[1]] if u % 2 == 0 else [ST[0], ST[1]]
            OK = [("PJ", 0), ("PJ", 1)] if u % 2 == 0 else [("ST", 0), ("ST", 1)]
            for kc in range(8):
                for half in range(2):
                    A("pe", lambda e, kc=kc, half=half, u=u, OB=OB: e.matmul(OB[half][0:tp, :],
                                                                           lhsT=yT[:, kc, u * 128:u * 128 + tp],
                                                                           rhs=Wob[:, kc, half * 512:(half + 1) * 512],
                                                                           start=(kc == 0), stop=(kc == 7)),
                      reads=[("yT", kc)], writes=[OK[half]])
            for half in range(2):
                A("dve", lambda e, half=half, xb=xb, OB=OB: e.tensor_tensor(out=XB[xb][0:tp, half * 512:(half + 1) * 512],
                                                                             in0=OB[half][0:tp, :],
                                                                             in1=XB[xb][0:tp, half * 512:(half + 1) * 512],
                                                                             op=ALU.add),
                  reads=[OK[half], xk], writes=[xk])
            st["pj"] = 0
            A("act", lambda e, u=u, xb=xb: e.activation(out=JUNK[0:tp, :], in_=XB[xb][0:tp, :], func=AF.Square,
                                                        accum_out=SS[0:tp, 4 + u:5 + u]),
              reads=[xk], writes=["JUNK", ("SS2", u)])
            A("act", lambda e, u=u: e.activation(out=LNT[0:tp, 4 + u:5 + u], in_=SS[0:tp, 4 + u:5 + u], func=AF.Ln,
                                                 bias=EPSC[0:tp, :], scale=1.0 / D), reads=[("SS2", u)], writes=[("LNT2", u)])
            A("act", lambda e, u=u: e.activation(out=RSTD[0:tp, 4 + u:5 + u], in_=LNT[0:tp, 4 + u:5 + u], func=AF.Exp,
                                                 scale=-0.5), reads=[("LNT2", u)], writes=[("RSTD2", u)])

    PJL = [PJ[0], PJ[1], ST[0], ST[1]]
    PJK = [("PJ", 0), ("PJ", 1), ("ST", 0), ("ST", 1)]

    def tile_mid(cfg, tp, nsub, ncols, iA0, iB0, colw, hkeys):

        def tokproj(u, c0, n, out_ap, okey):
            for kc in range(8):
                A("pe", lambda e, kc=kc: e.matmul(out_ap, lhsT=hT[:, kc, u * 128:u * 128 + tp],
                                                  rhs=Wbf[:, kc, c0:c0 + n], start=(kc == 0), stop=(kc == 7)),
                  reads=[("hT", u)], writes=[okey])

        for u in range(nsub):
            if not DBG.get('nozf'):
                tokproj(u, ZF0, 8, MISC[0:tp, u * 8:(u + 1) * 8], "MISC")
            jA = iA0 + u
            jB = iB0 + u
            b0 = nxt("pj", 4)
            tokproj(u, KA0, 512, PJL[b0][0:tp, :], PJK[b0])
            fa = ingest_k(PJL[b0][0:tp, :], PJK[b0], tp, "A", jA, cfg["ak_dst"](u), defer=True)
            b1 = nxt("pj", 4)
            tokproj(u, KB0, 512, PJL[b1][0:tp, :], PJK[b1])
            fb = ingest_k(PJL[b1][0:tp, :], PJK[b1], tp, "B", jB, cfg["bk_dst"](u), defer=True)
            b0 = nxt("pj", 4)
            tokproj(u, VA0, 512, PJL[b0][0:tp, :], PJK[b0])
            fa()
            ingest_v(PJL[b0][0:tp, :], PJK[b0], tp, "A", jA, cfg["av_dst"](u))
            b1 = nxt("pj", 4)
            tokproj(u, VB0, 512, PJL[b1][0:tp, :], PJK[b1])
            fb()
            ingest_v(PJL[b1][0:tp, :], PJK[b1], tp, "B", jB, cfg["bv_dst"](u))
            cfg.get("hook_sub", lambda u: None)(u)

        cfg.get("hook_post_p2", lambda: None)()
        if DBG['level'] < 3:
            return
        nz = nsub * 8
        A("dve", lambda e: e.tensor_tensor(out=ZT[0:tp, 0:nz].rearrange("p (u h) -> p u h", u=nsub),
                                           in0=MISC[0:tp, 0:nz].rearrange("p (u h) -> p u h", u=nsub),
                                           in1=_bmid(BFB[0:tp, :], nsub), op=ALU.add),
          reads=["MISC"], writes=["ZT"])
        A("act", lambda e: e.activation(out=ET[0:tp, 0:nz], in_=ZT[0:tp, 0:nz], func=AF.Exp, scale=-1.0),
          reads=["ZT"], writes=["ET"])
        A("act", lambda e: e.activation(out=ZT[0:tp, 0:nz], in_=ET[0:tp, 0:nz], func=AF.Ln, bias=ONES[0:tp, 0:1], scale=1.0),
          reads=["ET"], writes=["ZT"])
        A("dve", lambda e: e.tensor_scalar(out=LB[0:tp, 0:nsub, :].rearrange("p u h -> p (u h)"), in0=ZT[0:tp, 0:nz],
                                           scalar1=-1.0, scalar2=None, op0=ALU.mult), reads=["ZT"], writes=["LB"])
        store(LB[0:tp, 0:nsub, :], cfg["bf_dst"], ["LB"], "bf", "bf")
        def refs_and_bias():
            cumsum_blocks(nsub, iB0, tp)
            sel = SEL64 if tp == 128 else SELS
            A("pe", lambda e: e.matmul(MISC[:, 256:264], lhsT=sel[:], rhs=CK[:, iB0, :], start=True, stop=True),
              reads=["CK"], writes=["MISC"])
            i_last = iB0 + nsub - 1
            A("dve", lambda e: e.tensor_tensor(out=BIAS[:, 0:i_last + 1, :], in0=_bmid(MISC[:, 256:264], i_last + 1),
                                               in1=CK[:, 0:i_last + 1, :], op=ALU.subtract),
              reads=["MISC", "CK"], writes=["BIAS"])
            if nsub > 1:
                ng = 2 * nsub
                for u in range(nsub):
                    A("pe", lambda e, u=u: e.matmul(MISC[0:8, 300 + 2 * u:302 + 2 * u], lhsT=CK[:, iB0 + u, :],
                                                    rhs=E2[:, 0:2], start=True, stop=True), reads=["CK"], writes=["MISC"])
                A("dve", lambda e: e.tensor_scalar(out=TMP8[0:8, 0:ng], in0=MISC[0:8, 300:300 + ng],
                                                   scalar1=MISC[0:8, 300:301], scalar2=8.0, op0=ALU.subtract,
                                                   op1=ALU.mult), reads=["MISC"], writes=["TMP8"])
                A("dve", lambda e: e.tensor_copy(out=DT[0:8, 0:ng * 64].rearrange("p (g t) -> p g t", g=ng),
                                                 in_=bass.AP(TMP8, 0, [[8, 8], [1, ng], [0, 64]])),
                  reads=["TMP8"], writes=["DT"])

        if DBG['level'] < 4:
            return
        for ch in range(8):
            c0 = (GA0 if ch < 4 else GB0) + (ch % 4) * 128
            b = nxt("pj", 4)
            for kc in range(8):
                A("pe", lambda e, kc=kc, c0=c0, b=b: e.matmul(PJL[b][:, 0:ncols], lhsT=Wbf[:, kc, c0:c0 + 128],
                                                               rhs=hT[:, kc, 0:ncols], start=(kc == 0), stop=(kc == 7)),
                  reads=hkeys, writes=[PJK[b]])
            A("act", lambda e, ch=ch, b=b: e.activation(out=SG[:, ch, 0:ncols], in_=PJL[b][:, 0:ncols], func=AF.Silu),
              reads=[PJK[b]], writes=[("SG", ch)])

        if DBG['level'] < 5:
            return
        STL = [ST[0], ST[1], PJ[1], PJ[0]]
        STK = [("ST", 0), ("ST", 1), ("PJ", 1), ("PJ", 0)]

        bgq = []

        def emit_qproj(grp, c, spread=True):
            qs = c % 2
            q0 = (QA0 if grp == "A" else QB0) + c * 128
            b = 0

            TRf = TR[:].bitcast(F32)

            def mm(kcs):
                for kc in kcs:
                    A("pe", lambda e, kc=kc: e.matmul(TRf[:, 0:ncols], lhsT=Wbf[:, kc, q0:q0 + 128],
                                                      rhs=hT[:, kc, 0:ncols], start=(kc == 0), stop=(kc == 7)),
                      reads=hkeys, writes=["TR"])

            def ev():
                A("dve", lambda e: e.tensor_copy(out=QE[qs][0:64, 0:ncols], in_=TRf[0:64, 0:ncols]),
                  reads=["TR"], writes=[("QE", qs)])
                A("dve", lambda e: e.tensor_copy(out=QO[qs][64:128, 0:ncols], in_=TRf[64:128, 0:ncols]),
                  reads=["TR"], writes=[("QO", qs)])

            parts = [lambda: mm([0, 1]), lambda: mm([2, 3]), lambda: mm([4, 5]), lambda: mm([6, 7]), ev]
            if spread:
                bgq.extend(parts)
            else:
                for p_ in parts:
                    p_()

        def attn_group(grp):
            i0 = iA0 if grp == "A" else iB0
            items = []
            for c in range(4):
                for par in range(2):
                    if grp == "A":
                        js = [i0] + list(range(max(0, i0 - 4), i0)) + list(range(i0 + 1, i0 + nsub))
                    else:
                        js = list(range(0, i0 + nsub))
                    for n_, j in enumerate(js):
                        i_lo = max(j, i0)
                        i_hi = min(j + 4, i0 + nsub - 1) if grp == "A" else i0 + nsub - 1
                        items.append(dict(c=c, par=par, h=2 * c + par, j=j, first=(n_ == 0), last=(n_ == len(js) - 1),
                                          i_lo=i_lo, i_hi=i_hi, c0=(i_lo - i0) * colw, c1=(i_hi - i0 + 1) * colw))

            def emit_qk(it):
                c, par, h, j = it["c"], it["par"], it["h"], it["j"]
                qs = c % 2
                if it["first"] and par == 0:
                    while bgq:
                        bgq.pop(0)()
                if it["first"] and par == 1:
                    if c + 1 < 4:
                        emit_qproj(grp, c + 1)
                    elif grp == "A":
                        emit_qproj("B", 0)
                if it["first"]:
                    it_head = st["head"]
                    st["head"] += 1
                    it["ob"] = it_head % 2
                    cur["ob"] = it["ob"]
                else:
                    it["ob"] = cur["ob"]
                idx = st["item"]
                st["item"] += 1
                it["sb"] = idx % 4
                it["pt"] = idx % NPT
                sbk = it["sb"]
                STB = STL[sbk]
                stkey = STK[sbk]
                Q = QE[qs] if par == 0 else QO[qs]
                qk = ("QE", qs) if par == 0 else ("QO", qs)
                c0, c1 = it["c0"], it["c1"]
                if grp == "A":
                    kap = KaT[:, c, (j % 8) * 128:(j % 8) * 128 + 128]
                    kkey = ("KaT", j % 8)
                else:
                    kap = KbT[:, c, j * 128:(j + 1) * 128]
                    kkey = ("KbT", j)
                tabs = []
                for i in range(it["i_lo"], it["i_hi"] + 1):
                    cc0 = (i - i0) * colw
                    if grp == "A":
                        if i == j and i + 1 <= it["i_hi"] and colw == 128:
                            tabs.append((TB01[:, h, 0:256], cc0, 256))
                        elif i == j:
                            tabs.append((TB01[:, h, 0:colw], cc0, colw))
                        if i == j + 1 and not (j >= it["i_lo"] and colw == 128):
                            tabs.append((TB01[:, h, 128:128 + colw], cc0, colw))
                        if i == j + 4:
                            tabs.append((T4[:, 0:colw], cc0, colw))
                    else:
                        if i == j:
                            tabs.append((TDG[:, 0:colw], cc0, colw))
                useD = (grp == "B" and nsub > 1)
                A("pe", lambda e: e.matmul(STB[:, c0:c1], lhsT=kap, rhs=Q[:, c0:c1], start=True,
                                           stop=(len(tabs) == 0 and not useD)),
                  reads=[kkey, qk], writes=[stkey])
                if useD:
                    d0 = max(c0, 64)
                    A("pe", lambda e: e.matmul(STB[:, d0:c1], lhsT=ONESEL[:, h, :], rhs=DT[:, d0:c1], start=False,
                                               stop=(len(tabs) == 0)), reads=["DT"], writes=[stkey])
                for n_, (tap, cc0, tw) in enumerate(tabs):
                    A("pe", lambda e, tap=tap, cc0=cc0, n_=n_, tw=tw: e.matmul(STB[:, cc0:cc0 + tw], lhsT=Jb[:], rhs=tap,
                                                                      start=False, stop=(n_ == len(tabs) - 1)),
                      writes=[stkey])
                pt = it["pt"]
                if grp == "A":
                    A("act", lambda e: e.activation(out=PT[pt][:, c0:c1], in_=STB[:, c0:c1], func=AF.Exp, scale=0.125),
                      reads=[stkey], writes=[("PT", pt)])
                else:
                    A("act", lambda e: e.activation(out=PT[pt][:, c0:c1], in_=STB[:, c0:c1], func=AF.Exp,
                                                    bias=BIAS[:, j, h:h + 1], scale=0.125),
                      reads=[stkey, "BIAS"], writes=[("PT", pt)])

            def emit_pv(it):
                c, par, h, j = it["c"], it["par"], it["h"], it["j"]
                ob, pt = it["ob"], it["pt"]
                c0, c1 = it["c0"], it["c1"]
                if grp == "A":
                    VX, sl, vkey = VA, j % 8, ("VA", j % 8)
                else:
                    VX, sl, vkey = VB, j, ("VB", j)
                if par == 0:
                    lhs = VX[:, sl, c, 0:128]
                    oap = OT[ob][:, c0:c1]
                else:
                    lhs = VX[:, sl, c, 64:192]
                    oap = OT[ob][:, c0:c1]
                A("pe", lambda e: e.matmul(oap, lhsT=lhs, rhs=PT[pt][:, c0:c1], start=it["first"], stop=it["last"]),
                  reads=[vkey, ("PT", pt)], writes=[("OT", ob)])
                if it["last"]:
                    ch = (0 if grp == "A" else 4) + c
                    ysl = 0
                    if par == 0:
                        A("dve", lambda e: e.tensor_copy(out=RR[ysl][64:65, 0:ncols], in_=OT[ob][64:65, 0:ncols]),
                          reads=[("OT", ob)], writes=[("RR", ysl)])
                        A("dve", lambda e: e.tensor_tensor(out=Y1[ysl][0:64, 0:ncols], in0=OT[ob][0:64, 0:ncols],
                                                           in1=SG[0:64, ch, 0:ncols], op=ALU.mult),
                          reads=[("OT", ob), ("SG", ch)], writes=[("Y1", ysl)])
                    else:
                        A("dve", lambda e: e.tensor_copy(out=RR[ysl][0:1, 0:ncols], in_=OT[ob][0:1, 0:ncols]),
                          reads=[("OT", ob)], writes=[("RR", ysl)])
                        A("dve", lambda e: e.tensor_tensor(out=Y1[ysl][64:128, 0:ncols], in0=OT[ob][64:128, 0:ncols],
                                                           in1=SG[64:128, ch, 0:ncols], op=ALU.mult),
                          reads=[("OT", ob), ("SG", ch)], writes=[("Y1", ysl)])
                        def fin(ch=ch):
                            for q0 in range(0, ncols, 128):
                                q1 = min(ncols, q0 + 128)
                                A("pe", lambda e, q0=q0, q1=q1: e.matmul(MISC[:, q0:q1], lhsT=SEL2[:], rhs=RR[ysl][:, q0:q1],
                                                                         start=True, stop=True),
                                  reads=[("RR", ysl)], writes=["MISC"])
                            A("act", lambda e: e.activation(out=MISC[:, 0:ncols], in_=MISC[:, 0:ncols], func=AF.Ln),
                              reads=["MISC"], writes=["MISC"])
                            A("act", lambda e: e.activation(out=MISC[:, 0:ncols], in_=MISC[:, 0:ncols], func=AF.Exp,
                                                            scale=-1.0), reads=["MISC"], writes=["MISC"])
                            A("dve", lambda e: e.tensor_tensor(out=yT[:, ch, 0:ncols], in0=Y1[ysl][:, 0:ncols],
                                                               in1=MISC[:, 0:ncols], op=ALU.mult),
                              reads=[("Y1", ysl), "MISC"], writes=[("yT", ch)])

                        pending.append([2, fin])

            cur = {}
            SKEW = 3
            units = list(cfg.get("hook_units", [])) if grp == "B" else []
            idx0 = next((k for k, it in enumerate(items) if it["c"] == 3 and it["par"] == 0), len(items))
            step = max(1, (len(items) - idx0 - 1) // max(1, len(units)))
            pending = []
            for idx in range(len(items) + SKEW):
                if idx < len(items):
                    emit_qk(items[idx])
                for p in list(pending):
                    p[0] -= 1
                    if p[0] <= 0:
                        pending.remove(p)
                        p[1]()
                if idx >= SKEW:
                    emit_pv(items[idx - SKEW])
                if bgq:
                    bgq.pop(0)()
                if units and idx >= idx0 and (idx - idx0) % step == 0:
                    units.pop(0)()
            while units:
                units.pop(0)()
            for p in pending:
                p[1]()

        emit_qproj("A", 0, spread=False)
        refs_and_bias()
        attn_group("A")
        if DBG['level'] < 6:
            return
        attn_group("B")
        while bgq:
            bgq.pop(0)()
        if DBG['level'] < 7:
            return

    cfgs = []
    for s_ in range(nseq):
        for m in range(min(4, DBG['macros'])):
            r0 = s_ * S + m * 512

            def mk(s=s_, m=m, r0=r0):
                last = (m == 3)
                pre = (lambda: A("dve", lambda e: e.memset(LACC[:], 0.0), writes=["LACC"])) if m == 0 else (lambda: None)
                return dict(
                    tp=128, nsub=4, iA0=4 * m, iB0=4 * m, pre=pre,
                    x_src=lambda u: _dr(xp, r0 + u * 128, 128, 0, D, D),
                    y_dst=lambda u: _dr(yp, r0 + u * 128, 128, 0, D, D),
                    ak_dst=(lambda u: _dr(akp, s * 512 + u * 128, 128, 0, 512, 512)) if last else (lambda u: None),
                    av_dst=(lambda u: _dr(avp, s * 512 + u * 128, 128, 0, 512, 512)) if last else (lambda u: None),
                    bk_dst=lambda u: _dr(bkp, r0 + u * 128, 128, 0, 512, 512),
                    bv_dst=lambda u: _dr(bvp, r0 + u * 128, 128, 0, 512, 512),
                    bf_dst=bass.AP(bfp, r0 * 8, [[8, 128], [1024, 4], [1, 8]]),
                )

            cfgs.append(mk())

    def sample_pre():
        cfgs[-1].pop("prev_tail", lambda: None)()

        def load_kv(ksrc, vsrc):
            xb = nxt("xb", NXB)
            A("sp", lambda e: e.dma_start(out=XB[xb][:, 0:512], in_=ksrc), writes=[("XB", xb)], dma="x%d" % xb)
            A("sp", lambda e: e.dma_start(out=XB[xb][:, 512:1024], in_=vsrc), writes=[("XB", xb)], dma="x%d" % xb)
            return xb

        jobs = [("A", j, cak, cav) for j in range(4)] + [("B", j, cbk, cbv) for j in range(16)]
        slots = {}
        AHEAD = 3
        pend2 = None
        for n_ in range(len(jobs) + AHEAD):
            if n_ < len(jobs):
                g_, j_, kt, vt = jobs[n_]
                slots[n_] = load_kv(_dr(kt, j_ * 128, 128, 0, 512, 512), _dr(vt, j_ * 128, 128, 0, 512, 512))
            if n_ >= AHEAD:
                g_, j_, kt, vt = jobs[n_ - AHEAD]
                xb = slots[n_ - AHEAD]
                f2 = ingest_k(XB[xb][:, 0:512], ("XB", xb), 128, g_, j_, None, defer=True)
                if pend2 is not None:
                    pend2()
                pend2 = f2
                ingest_v(XB[xb][:, 512:1024], ("XB", xb), 128, g_, j_, None)
        if pend2 is not None:
            pend2()
        A("dve", lambda e: e.memset(VA[32:64, 4, :, :], 0.0), writes=[("VA", 4)])
        A("dve", lambda e: e.memset(VA[64:128, 4, :, :], 0.0), writes=[("VA", 4)])
        A("dve", lambda e: e.memset(VB[32:64, 16, :, :], 0.0), writes=[("VB", 16)])
        A("dve", lambda e: e.memset(VB[64:128, 16, :, :], 0.0), writes=[("VB", 16)])
        A("dve", lambda e: e.memset(LACC[:], 0.0), writes=["LACC"])
        A("sp", lambda e: e.dma_start(out=LB[:, :, :], in_=bass.AP(cbf, 0, [[8, 128], [1024, 16], [1, 8]])),
          writes=["LB"], dma="clf")
        cumsum_blocks(16, 0, 128)

    if with_sample and DBG['level'] >= 8:
        cfgs.append(dict(
            tp=TS, nsub=1, iA0=4, iB0=16, pre=sample_pre,
            x_src=lambda u: _dr(xs, 0, TS, 0, D, D),
            y_dst=lambda u: _dr(ys, 0, TS, 0, D, D),
            ak_dst=lambda u: _dr(aks, 0, TS, 0, 512, 512),
            av_dst=lambda u: _dr(avs, 0, TS, 0, 512, 512),
            bk_dst=lambda u: _dr(bks, 0, TS, 0, 512, 512),
            bv_dst=lambda u: _dr(bvs, 0, TS, 0, 512, 512),
            bf_dst=bass.AP(bfs, 0, [[8, TS], [8 * TS, 1], [1, 8]]),
        ))

    c0_ = cfgs[0]
    for u in range(c0_["nsub"]):
        tile_step(c0_, "p1load", u)
    for u in range(c0_["nsub"]):
        tile_step(c0_, "p1act", u)
    for u in range(c0_["nsub"]):
        tile_step(c0_, "p1pool", u)
        tile_step(c0_, "p1pe", u)
    for n, cfg in enumerate(cfgs):
        nx = cfgs[n + 1] if n + 1 < len(cfgs) else None

        def hook_sub(u, cfg=cfg, nx=nx):
            if u == 0:
                cfg.pop("prev_tail", lambda: None)()
                if nx is not None:
                    for v in range(nx["nsub"]):
                        tile_step(nx, "p1load", v)
            elif nx is not None and u - 1 < nx["nsub"]:
                tile_step(nx, "p1act", u - 1)
                nx.setdefault("_acted", set()).add(u - 1)

        def hook_post(cfg=cfg, nx=nx):
            if nx is not None:
                for v in range(nx["nsub"]):
                    if v not in nx.get("_acted", set()):
                        tile_step(nx, "p1act", v)
                        nx.setdefault("_acted", set()).add(v)

        cfg["hook_sub"] = hook_sub
        cfg["hook_post_p2"] = hook_post
        if nx is not None:
            def mk_unit(t, nx=nx, cfg=cfg):
                def unit():
                    if t == 0:
                        tile_step(nx, "p1pool", 0)
                        return
                    u = t - 1
                    tile_step(nx, "p1pe", u)
                    if u + 1 < nx["nsub"]:
                        tile_step(nx, "p1pool", u + 1)
                    if u < cfg["nsub"]:
                        tile_step(cfg, "p5load", u)
                return unit

            cfg["hook_units"] = [mk_unit(t) for t in range(nx["nsub"] + 1)]
        cfg["pre"]()
        tile_step(cfg, "mid")
        for u in range(cfg["nsub"]):
            if u not in cfg.get("_p5load", {}):
                tile_step(cfg, "p5load", u)
        for u in range(cfg["nsub"]):
            tile_step(cfg, "p5a", u)
            if u >= 1:
                tile_step(cfg, "p5b", u - 1)

        def tail(cfg=cfg):
            tile_step(cfg, "p5b", cfg["nsub"] - 1)

        if nx is None:
            tail()
        else:
            nx["prev_tail"] = tail

    A("sp", lambda e: None, reads=sorted(out_keys))

    P0.finalize()
    P1.finalize()
    sems = {}
    for n in P0.sem_names() + P1.sem_names():
        sems[n] = es.enter_context(nc.semaphore(n))
    with nc.allow_low_precision("bf16 operands by design"):
        with nc.Block() as block:
            P0.emit(block, sems)
        with nc.Block() as block:
            P1.emit(block, sems)
    es.close()
    return nc


_CACHE = {}


def _get_program(nseq):
    if nseq not in _CACHE:
        _CACHE[nseq] = build_program(nseq)
    return _CACHE[nseq]


def make_in_maps(inputs, nseq, ncores):
    f = lambda a: np.ascontiguousarray(np.asarray(a, dtype=np.float32))
    x_prompt = f(inputs["x_prompt"]); x_sample = f(inputs["x_sample"])
    in_maps = []
    for c in range(ncores):
        in_maps.append({
            "xp": x_prompt[c * nseq:(c + 1) * nseq].reshape(nseq * S, D),
            "xs": x_sample[c],
            "cak": f(inputs["cache_a_k"])[0, c].reshape(512, 512),
            "cav": f(inputs["cache_a_v"])[0, c].reshape(512, 512),
            "cbk": f(inputs["cache_b_k"])[0, c].reshape(S, 512),
            "cbv": f(inputs["cache_b_v"])[0, c].reshape(S, 512),
            "cbf": f(inputs["cache_b_logf"])[0, c].reshape(S, 8),
            "gain": f(inputs["norm_gain"]).reshape(1, D),
            "w_in": f(inputs["w_in"])[0],
            "bfg": f(inputs["b_forget"]).reshape(1, 8),
            "rb": f(inputs["rel_bias"])[0],
            "w_out": f(inputs["w_out"])[0],
            "fgn": f(inputs["final_gain"]).reshape(1, D),
        })
    return in_maps


def kernel(**inputs):
    nseq = NSEQ_CORE
    nc = _get_program(nseq)
    in_maps = make_in_maps(inputs, nseq, NCORES)
    res = run_bass_kernel_spmd(nc, in_maps, core_ids=list(range(NCORES)))
    R = res.results
    cat = lambda k: np.concatenate([r[k][None] if False else r[k] for r in R], axis=0)
    B = NCORES * nseq
    y_prompt = cat("yp").reshape(B, S, D)
    y_sample = np.stack([r["ys"] for r in R], 0)
    akp = cat("akp").reshape(1, B, 512, 8, 64)
    avp = cat("avp").reshape(1, B, 512, 8, 64)
    bkp = cat("bkp").reshape(1, B, S, 8, 64)
    bvp = cat("bvp").reshape(1, B, S, 8, 64)
    bfp = cat("bfp").reshape(1, B, S, 8)
    aks = np.stack([r["aks"] for r in R], 0).reshape(1, NCORES, TS, 8, 64)
    avs = np.stack([r["avs"] for r in R], 0).reshape(1, NCORES, TS, 8, 64)
    bks = np.stack([r["bks"] for r in R], 0).reshape(1, NCORES, TS, 8, 64)
    bvs = np.stack([r["bvs"] for r in R], 0).reshape(1, NCORES, TS, 8, 64)
    bfs = np.stack([r["bfs"] for r in R], 0).reshape(1, NCORES, TS, 8)
    return (y_prompt, y_sample, akp, avp, bkp, bvp, bfp, aks, avs, bks, bvs, bfs)
```
